# Optimizing a Trainium2 kernel written in Bass

```python
import math
import jax, jax.numpy as jnp
from jax import lax
import numpy as np

D_MODEL = 2048
BATCH = 4
SEQ = 4096
DEPTH = 2

N_A = DEPTH // 2
N_B = DEPTH - N_A

POOL_WINDOWS = (2, 4, 8, 16)
N_POOL_GROUPS = len(POOL_WINDOWS)
POOL_GROUP = D_MODEL // N_POOL_GROUPS

N_DIFF_HEADS = D_MODEL // 256
DIFF_HEAD_DIM = D_MODEL // N_DIFF_HEADS // 2
V_HEAD_DIM = 2 * DIFF_HEAD_DIM
ROT_DIM = DIFF_HEAD_DIM // 4
ROPE_THETA = 500000.0
Q_BLOCK = 128

D_FF = -(-8 * D_MODEL // (3 * 256)) * 256

EPS = 1e-6

kernel_name = "yoco_pool_diffattn_hybrid"


def rmsnorm(x, g):
    xf = x.astype(jnp.float32)
    y = xf * lax.rsqrt(jnp.mean(xf * xf, axis=-1, keepdims=True) + EPS)
    return (y * g.astype(jnp.float32)).astype(x.dtype)


def lambda_init_for(layer_idx):
    return 0.8 - 0.6 * math.exp(-0.3 * layer_idx)


def rope_tables(s):
    inv = ROPE_THETA ** (-jnp.arange(0, ROT_DIM, 2, dtype=jnp.float32) / ROT_DIM)
    ang = jnp.arange(s, dtype=jnp.float32)[:, None] * inv[None, :]
    return jnp.cos(ang), jnp.sin(ang)


def apply_partial_rope(x, cos, sin):
    half = ROT_DIM // 2
    xr = x[..., :ROT_DIM].astype(jnp.float32)
    x1, x2 = xr[..., :half], xr[..., half:]
    c = cos[None, :, None, :]
    s_ = sin[None, :, None, :]
    rot = jnp.concatenate([x1 * c - x2 * s_, x2 * c + x1 * s_], axis=-1)
    return jnp.concatenate([rot.astype(x.dtype), x[..., ROT_DIM:]], axis=-1)


def pool_mixer(h, w_pool, pool_scale):
    b, s, d = h.shape
    hf = h.astype(jnp.float32).reshape(b, s, N_POOL_GROUPS, POOL_GROUP)
    cs = jnp.cumsum(hf, axis=1)
    pos = jnp.arange(s)
    means = []
    for g, w in enumerate(POOL_WINDOWS):
        csg = cs[:, :, g]
        lagged = jnp.pad(csg, ((0, 0), (w, 0), (0, 0)))[:, :s]
        cnt = jnp.minimum(pos + 1, w).astype(jnp.float32)[None, :, None]
        means.append((csg - lagged) / cnt)
    mixed = (jnp.stack(means, axis=2) - hf).astype(h.dtype)
    y = jnp.einsum('bsgc,gcf->bsgf', mixed, w_pool)
    return y.reshape(b, s, d) * pool_scale


def shared_kv(x, kv_norm, w_kv, cos, sin):
    b, s, _ = x.shape
    kv = rmsnorm(x, kv_norm) @ w_kv
    k = kv[..., :D_MODEL].reshape(b, s, 2 * N_DIFF_HEADS, DIFF_HEAD_DIM)
    v = kv[..., D_MODEL:].reshape(b, s, N_DIFF_HEADS, V_HEAD_DIM)
    k = apply_partial_rope(k, cos, sin)
    return k.transpose(0, 2, 1, 3), v.transpose(0, 2, 1, 3)


def diff_attention(h, k, v, w_q, lam_q1, lam_k1, lam_q2, lam_k2, subln_gain, w_o,
                   lambda_init, cos, sin):
    b, s, _ = h.shape
    q = (h @ w_q).reshape(b, s, 2 * N_DIFF_HEADS, DIFF_HEAD_DIM)
    q = apply_partial_rope(q, cos, sin)
    lam = (jnp.exp(jnp.sum(lam_q1.astype(jnp.float32) * lam_k1.astype(jnp.float32)))
           - jnp.exp(jnp.sum(lam_q2.astype(jnp.float32) * lam_k2.astype(jnp.float32)))
           + lambda_init)
    scale = DIFF_HEAD_DIM ** -0.5
    nblk = s // Q_BLOCK
    qb = q.reshape(b, nblk, Q_BLOCK, 2 * N_DIFF_HEADS, DIFF_HEAD_DIM).transpose(1, 0, 3, 2, 4)
    kpos = jnp.arange(s)

    def one_block(args):
        qblk, i = args
        sc = jnp.einsum('bhqd,bhkd->bhqk', qblk, k,
                        preferred_element_type=jnp.float32) * scale
        qpos = i * Q_BLOCK + jnp.arange(Q_BLOCK)
        sc = jnp.where(kpos[None, :] <= qpos[:, None], sc, -jnp.inf)
        p = jax.nn.softmax(sc, axis=-1).reshape(b, N_DIFF_HEADS, 2, Q_BLOCK, s)
        a = p[:, :, 0] - lam * p[:, :, 1]
        return jnp.einsum('bhqk,bhkd->bhqd', a.astype(v.dtype), v)

    o = lax.map(one_block, (qb, jnp.arange(nblk)))
    o = o.transpose(1, 0, 3, 2, 4).reshape(b, s, N_DIFF_HEADS, V_HEAD_DIM)
    o = rmsnorm(o, subln_gain) * (1.0 - lambda_init)
    return o.reshape(b, s, D_MODEL) @ w_o


def swiglu(h, w_gate, w_up, w_down):
    return (jax.nn.silu(h @ w_gate) * (h @ w_up)) @ w_down


def setup_inputs(seed: int = 0) -> dict:
    key = jax.random.key(seed)
    ks = jax.random.split(key, 20)
    f32 = jnp.float32
    D = D_MODEL

    def gain(k, shape):
        return 1.0 + 0.05 * jax.random.normal(k, shape, f32)

    return {
        "x": jax.random.normal(ks[0], (BATCH, SEQ, D), f32),
        "norm_mix_pre": gain(ks[1], (DEPTH, D)),
        "norm_mix_post": gain(ks[2], (DEPTH, D)),
        "norm_ffn_pre": gain(ks[3], (DEPTH, D)),
        "norm_ffn_post": gain(ks[4], (DEPTH, D)),
        "w_pool": jax.random.normal(ks[5], (N_A, N_POOL_GROUPS, POOL_GROUP, POOL_GROUP), f32) * POOL_GROUP ** -0.5,
        "pool_scale": 1.0 + 0.1 * jax.random.normal(ks[6], (N_A, D), f32),
        "kv_norm": gain(ks[7], (D,)),
        "w_kv": jax.random.normal(ks[8], (D, 2 * D), f32) * D ** -0.5,
        "w_q": jax.random.normal(ks[9], (N_B, D, D), f32) * D ** -0.5,
        "lambda_q1": 0.1 * jax.random.normal(ks[10], (N_B, DIFF_HEAD_DIM), f32),
        "lambda_k1": 0.1 * jax.random.normal(ks[11], (N_B, DIFF_HEAD_DIM), f32),
        "lambda_q2": 0.1 * jax.random.normal(ks[12], (N_B, DIFF_HEAD_DIM), f32),
        "lambda_k2": 0.1 * jax.random.normal(ks[13], (N_B, DIFF_HEAD_DIM), f32),
        "subln_gain": gain(ks[14], (N_B, V_HEAD_DIM)),
        "w_o": jax.random.normal(ks[15], (N_B, D, D), f32) * D ** -0.5,
        "w_ffn_gate": jax.random.normal(ks[16], (DEPTH, D, D_FF), f32) * D ** -0.5,
        "w_ffn_up": jax.random.normal(ks[17], (DEPTH, D, D_FF), f32) * D ** -0.5,
        "w_ffn_down": jax.random.normal(ks[18], (DEPTH, D_FF, D), f32) * D_FF ** -0.5,
    }


def reference(x, norm_mix_pre, norm_mix_post, norm_ffn_pre, norm_ffn_post, w_pool, pool_scale,
              kv_norm, w_kv, w_q, lambda_q1, lambda_k1, lambda_q2, lambda_k2, subln_gain, w_o,
              w_ffn_gate, w_ffn_up, w_ffn_down):
    cos, sin = rope_tables(x.shape[1])
    k_sh, v_sh = None, None
    for l in range(DEPTH):
        if l == N_A:
            k_sh, v_sh = shared_kv(x, kv_norm, w_kv, cos, sin)
        h = rmsnorm(x, norm_mix_pre[l])
        if l < N_A:
            m = pool_mixer(h, w_pool[l], pool_scale[l])
        else:
            j = l - N_A
            m = diff_attention(h, k_sh, v_sh, w_q[j], lambda_q1[j], lambda_k1[j], lambda_q2[j],
                               lambda_k2[j], subln_gain[j], w_o[j], lambda_init_for(l), cos, sin)
        x = x + rmsnorm(m, norm_mix_post[l])
        h = rmsnorm(x, norm_ffn_pre[l])
        x = x + rmsnorm(swiglu(h, w_ffn_gate[l], w_ffn_up[l], w_ffn_down[l]), norm_ffn_post[l])
    return x
```

```python
import math
import os
from collections import defaultdict
from contextlib import ExitStack

import numpy as np
import concourse.bass as bass
import concourse.mybir as mybir
from concourse.bass_utils import run_bass_kernel_spmd

F32 = mybir.dt.float32
BF16 = mybir.dt.bfloat16
ALU = mybir.AluOpType
AF = mybir.ActivationFunctionType
AX = mybir.AxisListType

NCORES = 8
BATCH, SEQ, D, DFF = 4, 4096, 2048, 5632
KC, FC = D // 128, DFF // 128
T = 512
NT = 4
HALO = 16
TW = T + HALO
NSUB, NHEAD = 16, 8
EPS = 1e-6
ROT = 32
THETA = 500000.0
TILES = ([0, 3, 4, 7], [1, 2, 5, 6])
NKC = [8, 16, 24, 32]
NMASK = 8
LAMBDA_INIT = 0.8 - 0.6 * math.exp(-0.3 * 1)
SCALE = 128 ** -0.5
G_MIXPRE0, G_MIXPOST0, G_FFNPRE0, G_FFNPOST0, G_POOL, G_KV, G_MIXPRE1, G_MIXPOST1, G_FFNPRE1, G_FFNPOST1 = range(10)
NG = 10

SAME_ENGINE_SYNC = True
KSTOP = int(os.environ.get('KSTOP', '99'))
KNT = int(os.environ.get('KNT', '4'))


class Sched:
    ENG = ('pe', 'act', 'dve', 'pool', 'sp')

    def __init__(self, nc, stack, ndma=8, ndma_sp=48):
        self.nc = nc
        self.prog = {e: [] for e in self.ENG}
        self.sem = {}
        for e in ('pe', 'act', 'dve', 'pool'):
            self.sem[e] = stack.enter_context(nc.semaphore("c_" + e))
        self.seq = defaultdict(int)
        self.ndma = {'sp': ndma_sp, 'pool': ndma}
        for q in ('sp', 'pool'):
            for i in range(self.ndma[q]):
                self.sem[('d', q, i)] = stack.enter_context(nc.semaphore(f"d_{q}_{i}"))
        self.dma_n = {'sp': 0, 'pool': 0}
        self.known = {e: defaultdict(int) for e in self.ENG}
        self.last_w = {}
        self.readers = defaultdict(dict)
        self.nwaits = defaultdict(int)
        self.nops = defaultdict(int)

    def _deps(self, reads, writes):
        deps = {}

        def add(k, v):
            if deps.get(k, 0) < v:
                deps[k] = v
        for r in reads:
            for k, v in self.last_w.get(r, {}).items():
                add(k, v)
        for w in writes:
            for k, v in self.last_w.get(w, {}).items():
                add(k, v)
            for k, v in self.readers[w].items():
                add(k, v)
        return deps

    def alias(self, old, new):
        deps = self._deps((), old)
        for k in new:
            d = dict(self.last_w.get(k, {}))
            for kk, v in deps.items():
                if d.get(kk, 0) < v:
                    d[kk] = v
            self.last_w[k] = d

    def _emit_waits(self, eng, deps):
        for k, v in deps.items():
            if k == eng and (eng == 'pe' or not SAME_ENGINE_SYNC):
                continue
            if self.known[eng][k] >= v:
                continue
            sem = self.sem[k]
            self.prog[eng].append(lambda e, sem=sem, v=v: e.wait_ge(sem, v))
            self.known[eng][k] = v
            self.nwaits[eng] += 1

    def _commit(self, k, v, reads, writes):
        for r in reads:
            if self.readers[r].get(k, 0) < v:
                self.readers[r][k] = v
        for w in writes:
            self.last_w[w] = {k: v}
            self.readers[w] = {}

    def op(self, eng, fns, reads=(), writes=()):
        if callable(fns):
            fns = [fns]
        self._emit_waits(eng, self._deps(reads, writes))
        self.seq[eng] += 1
        v = self.seq[eng]
        sem = self.sem[eng]
        for f in fns[:-1]:
            self.prog[eng].append(f)
        last = fns[-1]
        self.prog[eng].append(lambda e, f=last, sem=sem: f(e).then_inc(sem, 1))
        self.nops[eng] += len(fns)
        self._commit(eng, v, reads, writes)

    def dma(self, q, fn, reads=(), writes=()):
        j = self.dma_n[q]
        self.dma_n[q] += 1
        nd = self.ndma[q]
        assert q != 'sp' or j < nd, "HWDGE (sp) DMA semaphores are never reused"
        k = ('d', q, j % nd)
        v = 16 * (j // nd + 1)
        deps = self._deps(reads, writes)
        if v > 16:
            deps[k] = max(deps.get(k, 0), v - 16)
        self._emit_waits(q, deps)
        sem = self.sem[k]
        self.prog[q].append(lambda e, f=fn, sem=sem: f(e).then_inc(sem, 16))
        self._commit(k, v, reads, writes)

    def coll(self, stack, fn, reads=(), writes=()):
        n = sum(1 for k in self.sem if isinstance(k, tuple) and k[0] == 'cc')
        k = ('cc', n)
        self.sem[k] = stack.enter_context(self.nc.semaphore(f"cc_{n}"))
        self._emit_waits('pool', self._deps(reads, writes))
        sem = self.sem[k]
        self.prog['pool'].append(lambda e, f=fn, sem=sem: f(e).then_inc(sem))
        self._commit(k, 1, reads, writes)

    def final_wait(self, eng):
        deps = {}
        for q, n in self.dma_n.items():
            nd = self.ndma[q]
            for i in range(min(n, nd)):
                cnt = (n - 1 - i) // nd + 1
                deps[('d', q, i)] = 16 * cnt
        self._emit_waits(eng, deps)

    def emit(self, block):
        for name, deco in (('pe', block.tensor), ('act', block.scalar), ('dve', block.vector),
                           ('pool', block.gpsimd), ('sp', block.sync)):
            prog = self.prog[name]

            def body(e, prog=prog):
                for f in prog:
                    f(e)
            deco(body)


class Prog:
    def __init__(self, mode):
        self.mode = mode
        self.nc = bass.Bass("TRN2", target_bir_lowering=False)

    def din(self, name, shape, dt=F32):
        return self.nc.dram_tensor(name, list(shape), dt, kind="ExternalInput").ap()

    def dout(self, name, shape, dt=F32):
        return self.nc.dram_tensor(name, list(shape), dt, kind="ExternalOutput").ap()

    def alloc(self, st):
        nc = self.nc
        self.S = Sched(nc, st, ndma=int(os.environ.get("KNDMA", "8")))
        A = lambda name, shape, dt: nc.alloc_sbuf_tensor('sb_' + name, shape, dt)
        self.xres = A("xres", [128, KC, T], F32)
        self.xh = A("xh", [128, KC, HALO], F32)
        self.wk = A("wk", [128, KC, TW], F32)
        self.hff = A("hff", [128, FC, T], BF16)
        self.ws = [A(f"ws{i}", [128, KC, T], BF16) for i in range(4)]
        self.gains = A("gains", [128, NG, KC], F32)
        self.rstd = A("rstd", [128, T], F32)
        self.rstdh = A("rstdh", [128, HALO], F32)
        self.sq = A("sq", [128, 4, T], BF16)
        self.pt = A("pt", [128, 4, T], BF16)
        self.tmp = A("tmp", [128, 2, T], F32)
        self.cstA = A("cstA", [128, 4, T], F32)
        self.cstB = A("cstB", [128, 2, T], F32)
        self.ones = A("ones", [128, 128], BF16)
        self.ident = A("ident", [128, 128], F32)
        self.Rm = A("Rm", [128, 128], BF16)
        self.gsub = A("gsub", [128, 256], F32)
        self.eps = A("eps", [128, 1], F32)
        self.sm = A("sm", [128, 32], F32)
        self.ps = [nc.alloc_psum_tensor(f"ps{i}", [128, T], F32) for i in range(8)]
        wkflat = self.wk[:].rearrange("p c t -> p (c t)")
        wkb16 = wkflat.bitcast(BF16)
        self.wkb = wkb16[:, 0:KC * T].rearrange("p (c t) -> p c t", c=KC)
        self.vb = [wkb16[:, i * 8448:(i + 1) * 8448].rearrange("p (c w) -> p c w", w=264) for i in range(2)]
        ws0 = self.ws[0][:].rearrange("p c t -> p (c t)")
        self.kb = [ws0[:, i * 4096:(i + 1) * 4096] for i in range(2)]
        self.maskb = self.cstA[:].rearrange("p c t -> p (c t)").bitcast(BF16).rearrange("p (c t) -> p c t", c=8)
        hf32 = self.hff[:].rearrange("p c t -> p (c t)").bitcast(F32)
        self.pt_a = hf32[:, 0:4 * TW].rearrange("p (c t) -> p c t", c=4)
        self.pt_b = hf32[:, 4 * TW:8 * TW].rearrange("p (c t) -> p c t", c=4)
        self.o0 = self.sq[:].rearrange("p c t -> p (c t)").bitcast(F32).rearrange("p (c t) -> p c t", c=4)
        self.on = self.tmp[:].rearrange("p c t -> p (c t)").rearrange("p (c t) -> p c t", c=4)
        self.wslot_rr = 0
        self.sqi = 0
        self.psi = 0
        self.pti = 0

    @staticmethod
    def X(c=None):
        return [('x', i) for i in range(KC)] if c is None else [('x', c)]

    @staticmethod
    def WK(c=None):
        return [('wk', i) for i in range(KC)] if c is None else [('wk', c)]

    WKB = ['wkb']
    PT = [('pt', i) for i in range(4)]

    def wload(self, src, nk, allowed=(0, 1, 2, 3)):
        while self.wslot_rr % 4 not in allowed:
            self.wslot_rr += 1
        s = self.wslot_rr % 4
        self.wslot_rr += 1
        dst = self.ws[s]
        self.S.dma('pool', lambda e: e.dma_start(out=dst[:, 0:nk, :], in_=src), writes=[('ws', s)])
        return s

    def bank(self, n=6):
        b = self.psi % n
        self.psi += 1
        return b

    def ss_acc(self, src, keys, N, c, nch=KC, scale=None):
        S = self.S
        i = self.sqi % 4
        self.sqi += 1
        if scale is None:
            S.op('act', lambda e: e.activation(out=self.sq[:, i, 0:N], in_=src, func=AF.Square),
                 reads=keys, writes=[('sq', i)])
        else:
            S.op('act', lambda e: e.activation(out=self.sq[:, i, 0:N], in_=src, func=AF.Square, scale=scale),
                 reads=keys + ['gains'], writes=[('sq', i)])
        S.op('pe', lambda e: e.matmul(self.ps[6][:, 0:N], self.ones[:], self.sq[:, i, 0:N], start=(c == 0), stop=(c == nch - 1)),
             reads=[('sq', i), 'ones'], writes=[('ps', 6)])

    def ss_fin(self, N, rstd, rkey, denom=D):
        S = self.S
        S.op('act', lambda e: e.activation(out=rstd[:, 0:N], in_=self.ps[6][:, 0:N], func=AF.Sqrt, bias=self.eps[:], scale=1.0 / denom),
             reads=[('ps', 6), 'eps'], writes=[rkey])
        S.op('dve', lambda e: e.reciprocal(out=rstd[:, 0:N], in_=rstd[:, 0:N]), reads=[rkey], writes=[rkey])

    def sumsq(self, src_fn, keys_fn, N, rstd, rkey, nch=KC, denom=D, scale_fn=None):
        for c in range(nch):
            self.ss_acc(src_fn(c), keys_fn(c), N, c, nch, None if scale_fn is None else scale_fn(c))
        self.ss_fin(N, rstd, rkey, denom)

    def rms_to(self, src_fn, skeys_fn, gidx, dst_fn, dkeys_fn, N, rstd, rkey):
        for c in range(KC):
            src, dst = src_fn(c), dst_fn(c)
            self.S.op('dve', lambda e, src=src, dst=dst, c=c: e.scalar_tensor_tensor(
                out=dst, in0=src, scalar=self.gains[:, gidx, c:c + 1], in1=rstd[:, 0:N], op0=ALU.mult, op1=ALU.mult),
                reads=skeys_fn(c) + [rkey, 'gains'], writes=dkeys_fn(c))

    def wkm(self, c):
        return self.wk[:, c, HALO:TW]

    def residual_add(self, gidx, want_ss=True):
        S = self.S
        for c in range(KC):
            S.op('dve', lambda e, c=c: e.scalar_tensor_tensor(
                out=self.wkm(c), in0=self.wkm(c), scalar=self.gains[:, gidx, c:c + 1], in1=self.rstd[:, 0:T],
                op0=ALU.mult, op1=ALU.mult), reads=self.WK(c) + ['rstd', 'gains'], writes=self.WK(c))
        for c in range(KC):
            S.op('dve', lambda e, c=c: e.tensor_tensor(out=self.xres[:, c, :], in0=self.xres[:, c, :], in1=self.wkm(c), op=ALU.add),
                 reads=self.WK(c) + self.X(c), writes=self.X(c))
        if want_ss:
            for c in range(KC):
                self.ss_acc(self.xres[:, c, :], self.X(c), T, c)
        self.x_ss_ready = want_ss

    def norm_x_to_wkb(self, gidx):
        S = self.S
        if getattr(self, 'x_ss_ready', False):
            self.ss_fin(T, self.rstd, 'rstd')
            self.x_ss_ready = False
        else:
            self.sumsq(lambda c: self.xres[:, c, :], lambda c: self.X(c), T, self.rstd, 'rstd')
        S.alias(self.WK(), self.WKB)
        self.rms_to(lambda c: self.xres[:, c, :], lambda c: self.X(c), gidx,
                    lambda c: self.wkb[:, c, :], lambda c: self.WKB, T, self.rstd, 'rstd')

    def ffn(self, l, wg, wu, wd, g_pre, g_post, last=False):
        S = self.S
        self.norm_x_to_wkb(g_pre)
        for b in range(FC // 4):
            sg = self.wload(wg[b], KC)
            su = self.wload(wu[b], KC)
            for fc in range(4):
                pa, pb = self.bank(), self.bank()
                for (s_, p_) in ((sg, pa), (su, pb)):
                    fns = [lambda e, s_=s_, p_=p_, kc=kc, fc=fc: e.matmul(
                        self.ps[p_][:], self.ws[s_][:, kc, fc * 128:(fc + 1) * 128], self.wkb[:, kc, :],
                        start=(kc == 0), stop=(kc == KC - 1)) for kc in range(KC)]
                    S.op('pe', fns, reads=[('ws', s_)] + self.WKB, writes=[('ps', p_)])
                ti = (b * 4 + fc) % 2
                S.op('act', lambda e, pa=pa, ti=ti: e.activation(out=self.tmp[:, ti, :], in_=self.ps[pa][:], func=AF.Silu),
                     reads=[('ps', pa)], writes=[('tmp', ti)])
                S.op('dve', lambda e, pb=pb, ti=ti, f=b * 4 + fc: e.tensor_tensor(
                    out=self.hff[:, f, :], in0=self.tmp[:, ti, :], in1=self.ps[pb][:], op=ALU.mult),
                    reads=[('tmp', ti), ('ps', pb)], writes=[('hf', b * 4 + fc)])
        S.alias(self.WKB, self.WK())
        parts = ((0, 16), (16, 32), (32, 44))
        for nb in range(4):
            banks = [0, 1, 2, 3] if nb % 2 == 0 else [4, 5, 6, 7]
            for pi, (k0, k1) in enumerate(parts):
                s_ = self.wload(wd[nb, :, k0:k1, :], k1 - k0)
                for dc in range(4):
                    fns = [lambda e, s_=s_, kk=kk, dc=dc, k0=k0, p_=banks[dc], pi=pi, k1=k1: e.matmul(
                        self.ps[p_][:], self.ws[s_][:, kk, dc * 128:(dc + 1) * 128], self.hff[:, k0 + kk, :],
                        start=(pi == 0 and kk == 0), stop=(pi == 2 and kk == k1 - k0 - 1)) for kk in range(k1 - k0)]
                    S.op('pe', fns, reads=[('ws', s_)] + [('hf', k0 + kk) for kk in range(k1 - k0)], writes=[('ps', banks[dc])])
            for dc in range(4):
                c = nb * 4 + dc
                S.op('act', lambda e, c=c, p_=banks[dc]: e.copy(out=self.wkm(c), in_=self.ps[p_][:]),
                     reads=[('ps', banks[dc])], writes=self.WK(c))
        self.sumsq(lambda c: self.wkm(c), lambda c: self.WK(c), T, self.rstd, 'rstd')
        self.residual_add(g_post, want_ss=not last)

    def rope(self, p_, dst, dkeys):
        S = self.S
        i = self.pti % 2
        self.pti += 1
        rb = self.pt[:, 2 + i, :]
        KR = os.environ.get('KROPE', '3')
        if KR == '3':
            S.op('dve', lambda e: e.tensor_copy(out=rb, in_=self.ps[p_][:]), reads=[('ps', p_)], writes=[('pt', 2 + i)])
        else:
            S.op('act', lambda e: e.copy(out=rb, in_=self.ps[p_][:]), reads=[('ps', p_)], writes=[('pt', 2 + i)])
        S.op('dve', lambda e: e.tensor_tensor(out=self.tmp[:, i, :], in0=self.ps[p_][:], in1=self.cstB[:, 0, :], op=ALU.mult),
             reads=[('ps', p_), 'cstB'], writes=[('tmp', i)])
        if KR != '2':
            S.op('pe', lambda e: e.matmul(self.ps[7][:], self.Rm[:], rb, start=True, stop=True),
                 reads=[('pt', 2 + i), 'Rm'], writes=[('ps', 7)])
        if KR == '4':
            S.op('act', lambda e: e.copy(out=dst, in_=self.ps[7][:]), reads=[('ps', 7)], writes=dkeys)
            return
        S.op('dve', lambda e: e.tensor_tensor(out=self.rstd[:, :], in0=self.ps[7][:], in1=self.cstB[:, 1, :], op=ALU.mult),
             reads=[('ps', 7), 'cstB'], writes=['rstd'])
        S.op('dve', lambda e: e.tensor_tensor(out=dst, in0=self.tmp[:, i, :], in1=self.rstd[:, :], op=ALU.add),
             reads=[('tmp', i), 'rstd'], writes=dkeys)

    def init_consts(self, gains_d, ident_d=None, R_d=None):
        S = self.S
        S.dma('sp', lambda e: e.dma_start(out=self.gains[:], in_=gains_d), writes=['gains'])
        S.op('dve', lambda e: e.memset(self.ones[:], 1.0), writes=['ones'])
        S.op('dve', lambda e: e.memset(self.eps[:], EPS), writes=['eps'])
        if ident_d is not None:
            S.dma('sp', lambda e: e.dma_start(out=self.ident[:], in_=ident_d), writes=['ident'])
        if R_d is not None:
            S.dma('pool', lambda e: e.dma_start(out=self.Rm[:], in_=R_d), writes=['Rm'])

    def build_A(self):
        nc = self.nc
        xT = self.din("xT", [NT, 128, KC, TW])
        gains_d = self.din("gains", [128, NG, KC])
        invc_d = self.din("invc", [NT, 128, 4, T])
        rope_d = self.din("rope", [NT, 128, 2, T])
        R_d = self.din("Rmat", [128, 128])
        wp = self.din("wpool", [4, 128, 4, T])
        wg = self.din("wgate", [FC // 4, 128, KC, T])
        wu = self.din("wup", [FC // 4, 128, KC, T])
        wd = self.din("wdown", [4, 128, FC, T])
        wkv = self.din("wkv", [8, 128, KC, T])
        x2 = self.dout("x2T", [NT, 128, KC, T])
        Ko = self.dout("Kout", [NT, NSUB, 128, T // 2]).bitcast(BF16)
        Vo = self.dout("Vout", [NT, NHEAD, 128, 4, 128]).bitcast(BF16)
        with ExitStack() as st:
            self.alloc(st)
            S = self.S
            self.init_consts(gains_d, None, R_d)
            for s in range(KNT):
                self.layer0_tile(s, xT, invc_d, rope_d, wp, wg, wu, wd, wkv, x2, Ko, Vo)
            S.final_wait('sp')
            with nc.Block() as block:
                S.emit(block)
        return nc

    def load_x0(self, s, xT, invc_d):
        S = self.S
        S.dma('sp', lambda e: e.dma_start(out=self.xres[:], in_=xT[s, :, :, HALO:TW]), writes=self.X())
        S.dma('sp', lambda e: e.dma_start(out=self.xh[:], in_=xT[s, :, :, 0:HALO]), writes=['xh'])
        S.dma('sp', lambda e: e.dma_start(out=self.cstA[:], in_=invc_d[s]), writes=['cstA'])

    def layer0_tile(self, s, xT, invc_d, rope_d, wp, wg, wu, wd, wkv, x2, Ko, Vo, after_kvnorm=None):
        S = self.S
        if s == 0:
            self.load_x0(s, xT, invc_d)
        self.sumsq(lambda c: self.xres[:, c, :], lambda c: self.X(c), T, self.rstd, 'rstd')
        self.rms_to(lambda c: self.xres[:, c, :], lambda c: self.X(c), G_MIXPRE0,
                    lambda c: self.wkm(c), lambda c: self.WK(c), T, self.rstd, 'rstd')
        self.sumsq(lambda c: self.xh[:, c, :], lambda c: ['xh'], HALO, self.rstdh, 'rstdh')
        self.rms_to(lambda c: self.xh[:, c, :], lambda c: ['xh'], G_MIXPRE0,
                    lambda c: self.wk[:, c, 0:HALO], lambda c: self.WK(c), HALO, self.rstdh, 'rstdh')
        if KSTOP <= 1:
            return self.dbg_out(s, x2)
        HT = [('hf', i) for i in range(17)]
        for g in range(4):
            h4 = self.wk[:, 4 * g:4 * g + 4, :]
            cur, curk = h4, [('wk', 4 * g + i) for i in range(4)]
            bufs = [(self.pt_a, 'pta'), (self.pt_b, 'ptb')]
            bi = 0
            for step in range(g + 1):
                sh = 1 << step
                dst, dk = bufs[bi]
                bi ^= 1
                S.op('dve', lambda e, dst=dst, cur=cur, sh=sh: e.tensor_tensor(
                    out=dst[:, :, sh:TW], in0=cur[:, :, sh:TW], in1=cur[:, :, 0:TW - sh], op=ALU.add),
                    reads=curk + HT, writes=[dk] + HT)
                cur, curk = dst, [dk]
            oth, ok = bufs[bi]
            for i in range(4):
                c = 4 * g + i
                S.op('dve', lambda e, cur=cur, i=i, c=c, g=g: e.scalar_tensor_tensor(
                    out=self.hff[:, 20 + c, :], in0=cur[:, i, HALO:TW], scalar=1.0 / (2 << g), in1=self.wkm(c),
                    op0=ALU.mult, op1=ALU.subtract),
                    reads=curk + self.WK(c) + HT, writes=[('hf', 20 + c)])
            for i in range(4):
                c = 4 * g + i
                S.op('dve', lambda e, oth=oth, cur=cur, i=i, g=g: e.tensor_tensor(
                    out=oth[:, i, HALO:2 * HALO], in0=cur[:, i, HALO:2 * HALO], in1=self.cstA[:, g, 0:HALO], op=ALU.mult),
                    reads=curk + ['cstA'] + HT, writes=[ok] + HT)
                S.op('dve', lambda e, oth=oth, i=i, c=c: e.tensor_tensor(
                    out=self.hff[:, 20 + c, 0:HALO], in0=oth[:, i, HALO:2 * HALO], in1=self.wk[:, c, HALO:2 * HALO], op=ALU.subtract),
                    reads=[ok] + self.WK(c) + HT, writes=[('hf', 20 + c)])
        if KSTOP <= 2:
            return self.dbg_out(s, x2)
        for g in range(4):
            s_ = self.wload(wp[g], 4)
            for oc in range(4):
                c = 4 * g + oc
                p_ = self.bank()
                fns = [lambda e, s_=s_, kc=kc, oc=oc, p_=p_, g=g: e.matmul(
                    self.ps[p_][:], self.ws[s_][:, kc, oc * 128:(oc + 1) * 128], self.hff[:, 20 + 4 * g + kc, :],
                    start=(kc == 0), stop=(kc == 3)) for kc in range(4)]
                S.op('pe', fns, reads=[('ws', s_)] + [('hf', 20 + 4 * g + kc) for kc in range(4)], writes=[('ps', p_)])
                S.op('act', lambda e, c=c, p_=p_: e.activation(out=self.wkm(c), in_=self.ps[p_][:], func=AF.Copy,
                                                              scale=self.gains[:, G_POOL, c:c + 1]),
                     reads=[('ps', p_), 'gains'], writes=self.WK(c))
                self.ss_acc(self.ps[p_][:], [('ps', p_)], T, c, scale=self.gains[:, G_POOL, c:c + 1])
        self.ss_fin(T, self.rstd, 'rstd')
        self.residual_add(G_MIXPOST0)
        if KSTOP <= 3:
            return self.dbg_out(s, x2)
        self.ffn(0, wg, wu, wd, G_FFNPRE0, G_FFNPOST0)
        S.dma('sp', lambda e: e.dma_start(out=x2[s], in_=self.xres[:]), reads=self.X(), writes=[('x2', s)])
        if KSTOP <= 5:
            return
        S.dma('sp', lambda e: e.dma_start(out=self.cstB[:], in_=rope_d[s]), writes=['cstB'])
        self.norm_x_to_wkb(G_KV)
        if after_kvnorm is not None:
            after_kvnorm()
        if KSTOP > 6:
            self.kvproj(s, wkv, Ko, Vo)
        S.alias(self.WKB, self.WK())


    def kvproj(self, s, wkv, Ko, Vo):
        S = self.S

        def krope(j, p_):
            si = j % 2
            self.rope(p_, self.pt[:, si, :], [('pt', si)])
            S.dma('pool', lambda e: e.dma_start(out=Ko[j * 128:(j + 1) * 128, :], in_=self.pt[:, si, :]),
                  reads=[('pt', si)], writes=[('Ko', s, j)])
        kpend = None
        slots = {0: self.wload(wkv[0], KC)}
        for blk in range(8):
            if blk + 1 < 8:
                slots[blk + 1] = self.wload(wkv[blk + 1], KC)
            s_ = slots[blk]
            if blk < 4:
                nb = blk
                for oc in range(4):
                    j = nb * 4 + oc
                    p_ = self.bank()
                    fns = [lambda e, s_=s_, kc=kc, oc=oc, p_=p_: e.matmul(
                        self.ps[p_][:], self.ws[s_][:, kc, oc * 128:(oc + 1) * 128], self.wkb[:, kc, :],
                        start=(kc == 0), stop=(kc == KC - 1)) for kc in range(KC)]
                    S.op('pe', fns, reads=[('ws', s_)] + self.WKB, writes=[('ps', p_)])
                    if kpend is not None:
                        krope(*kpend)
                    kpend = (j, p_)
                if blk == 3:
                    krope(*kpend)
            elif os.environ.get('KV', '1') == '1':
                nb = blk - 4
                for tc in range(4):
                    p_ = self.bank()
                    fns = [lambda e, s_=s_, kc=kc, tc=tc, p_=p_: e.matmul(
                        self.ps[p_][:], self.wkb[:, kc, tc * 128:(tc + 1) * 128], self.ws[s_][:, kc, :],
                        start=(kc == 0), stop=(kc == KC - 1)) for kc in range(KC)]
                    S.op('pe', fns, reads=[('ws', s_)] + self.WKB, writes=[('ps', p_)])
                    si = tc % 2
                    S.op('act', lambda e, p_=p_, si=si: e.copy(out=self.pt[:, si, :], in_=self.ps[p_][:]),
                         reads=[('ps', p_)], writes=[('pt', si)])
                    S.dma('pool', lambda e, nb=nb, tc=tc, si=si: e.dma_start(
                        out=Vo[2 * nb:2 * nb + 2, :, tc, :].rearrange("h p d -> p h d"),
                        in_=self.pt[:, si, :].rearrange("p (h d) -> p h d", h=2)), reads=[('pt', si)], writes=[('Vo', s, nb, tc)])

    def dbg_out(self, s, x2):
        self.S.dma('sp', lambda e: e.dma_start(out=x2[s], in_=self.xres[:]), reads=self.X(), writes=[('x2', s)])

    def build_F(self):
        nc = self.nc
        xT = self.din("xT", [NT, 128, KC, TW])
        gains_d = self.din("gains", [128, NG, KC])
        invc_d = self.din("invc", [NT, 128, 4, T])
        rope_d = self.din("rope", [NT, 128, 2, T])
        mask_d = self.din("maskd", [NT, 128, NMASK, T])
        R_d = self.din("Rmat", [128, 128])
        ident_d = self.din("ident", [128, 128])
        gsub_d = self.din("gsub", [128, 256])
        lam_d = self.din("lamv", [128, 4, 128])
        wp = self.din("wpool", [4, 128, 4, T])
        wg = [self.din(f"wgate{l}", [FC // 4, 128, KC, T]) for l in range(2)]
        wu = [self.din(f"wup{l}", [FC // 4, 128, KC, T]) for l in range(2)]
        wd = [self.din(f"wdown{l}", [4, 128, FC, T]) for l in range(2)]
        wkv = self.din("wkv", [8, 128, KC, T])
        wq = self.din("wq", [4, 128, KC, T])
        wo = self.din("wo", [4, 128, KC, T])
        yT = self.dout("yT", [NT, 128, KC, T])
        x2 = nc.dram_tensor("x2s", [NT, 128, KC, T], F32).ap()
        Kloc = [nc.dram_tensor(f"Kloc{s}", [NSUB * 128, T], BF16).ap() for s in range(NT)]
        Vloc = [nc.dram_tensor(f"Vloc{s}", [NHEAD * 128, 4 * 256], BF16).ap() for s in range(NT)]
        Kall = [nc.dram_tensor(f"Kall{s}", [2 * NSUB * 128, T], BF16).ap() for s in range(NT)]
        Vall = [nc.dram_tensor(f"Vall{s}", [2 * NHEAD * 128, 4 * 256], BF16).ap() for s in range(NT)]
        Vloc_v = [v.rearrange("(h p) (c d) -> h p c d", p=128, d=256) for v in Vloc]
        Vall_v = [v.rearrange("(g p) (c d) -> g p c d", p=128, d=256) for v in Vall]
        groups = [[2 * i, 2 * i + 1] for i in range(NCORES // 2)]
        with ExitStack() as st:
            self.alloc(st)
            S = self.S
            self.init_consts(gains_d, ident_d, R_d)
            S.dma('sp', lambda e: e.dma_start(out=self.gsub[:], in_=gsub_d), writes=['gsub'])
            for s in range(NT):
                if s + 1 < NT:
                    nxt = lambda s=s: self.load_x0(s + 1, xT, invc_d)
                else:
                    nxt = lambda: S.dma('sp', lambda e: e.dma_start(out=self.xres[:], in_=x2[0]), reads=[('x2', 0)], writes=self.X())
                self.layer0_tile(s, xT, invc_d, rope_d, wp, wg[0], wu[0], wd[0], wkv, x2, Kloc[s], Vloc_v[s], after_kvnorm=nxt)
                kkeys = [('Ko', s, j) for j in range(NSUB)]
                vkeys = [('Vo', s, nb, tc) for nb in range(4) for tc in range(4)]
                S.coll(st, lambda e, s=s: e.collective_compute("AllGather", ALU.bypass, replica_groups=groups,
                                                               ins=[Kloc[s].opt()], outs=[Kall[s].opt()]), reads=kkeys, writes=[('Kall', s)])
                S.coll(st, lambda e, s=s: e.collective_compute("AllGather", ALU.bypass, replica_groups=groups,
                                                               ins=[Vloc[s].opt()],
                                                               outs=[Vall[s].opt()]), reads=vkeys, writes=[('Vall', s)])
            self.compute_lambda(lam_d)
            for s in range(NT):
                self.layer1_tile(s, x2, rope_d, mask_d, Kall, Vall_v, wq, wo, wg[1], wu[1], wd[1], yT)
            S.final_wait('sp')
            with nc.Block() as block:
                S.emit(block)
        return nc

    def build_B(self):
        nc = self.nc
        x2 = self.din("x2T", [NT, 128, KC, T])
        gains_d = self.din("gains", [128, NG, KC])
        rope_d = self.din("rope", [NT, 128, 2, T])
        mask_d = self.din("maskd", [NT, 128, NMASK, T])
        R_d = self.din("Rmat", [128, 128])
        ident_d = self.din("ident", [128, 128])
        gsub_d = self.din("gsub", [128, 256])
        lam_d = self.din("lamv", [128, 4, 128])
        Kd = self.din("Kd", [NSUB, 128, SEQ // 2]).bitcast(BF16)
        Vd = self.din("Vd", [NHEAD, 128, SEQ // 128, 128]).bitcast(BF16)
        wq = self.din("wq", [4, 128, KC, T])
        wo = self.din("wo", [4, 128, KC, T])
        wg = self.din("wgate", [FC // 4, 128, KC, T])
        wu = self.din("wup", [FC // 4, 128, KC, T])
        wd = self.din("wdown", [4, 128, FC, T])
        yT = self.dout("yT", [NT, 128, KC, T])
        with ExitStack() as st:
            self.alloc(st)
            S = self.S
            self.init_consts(gains_d, ident_d, R_d)
            S.dma('sp', lambda e: e.dma_start(out=self.gsub[:], in_=gsub_d), writes=['gsub'])
            self.compute_lambda(lam_d)
            for s in range(KNT):
                self.layer1_tile(s, x2, rope_d, mask_d, Kd, Vd, wq, wo, wg, wu, wd, yT)
            S.final_wait('sp')
            with nc.Block() as block:
                S.emit(block)
        return nc

    def compute_lambda(self, lam_d):
        S = self.S
        lv = self.cstB[:].rearrange("p c t -> p (c t)")[:, 0:512].rearrange("p (c t) -> p c t", c=4)
        S.dma('sp', lambda e: e.dma_start(out=lv, in_=lam_d), writes=['cstB'])
        for i in range(2):
            S.op('dve', lambda e, i=i: e.tensor_tensor(out=self.tmp[:, 0, i * 128:(i + 1) * 128], in0=lv[:, 2 * i, :], in1=lv[:, 2 * i + 1, :], op=ALU.mult),
                 reads=['cstB'], writes=[('tmp', 0)])
            S.op('dve', lambda e, i=i: e.reduce_sum(out=self.sm[:, 2 + i:3 + i], in_=self.tmp[:, 0, i * 128:(i + 1) * 128], axis=AX.X),
                 reads=[('tmp', 0)], writes=['sm'])
        S.op('act', lambda e: e.activation(out=self.sm[:, 4:6], in_=self.sm[:, 2:4], func=AF.Exp), reads=['sm'], writes=['sm'])
        S.op('dve', lambda e: e.tensor_tensor(out=self.sm[:, 0:1], in0=self.sm[:, 5:6], in1=self.sm[:, 4:5], op=ALU.subtract),
             reads=['sm'], writes=['sm'])
        S.op('dve', lambda e: e.tensor_scalar_add(out=self.sm[:, 0:1], in0=self.sm[:, 0:1], scalar1=-LAMBDA_INIT), reads=['sm'], writes=['sm'])

    def layer1_tile(self, s, x2, rope_d, mask_d, Kd, Vd, wq, wo, wg, wu, wd, yT):
        S = self.S
        nkc = NKC[s]
        half = nkc // 2
        if s > 0:
            S.dma('sp', lambda e: e.dma_start(out=self.xres[:], in_=x2[s]), reads=[('x2', s)], writes=self.X())
        S.dma('sp', lambda e: e.dma_start(out=self.cstB[:], in_=rope_d[s]), writes=['cstB'])
        S.dma('pool', lambda e: e.dma_start(out=self.maskb, in_=mask_d[s]), writes=['cstA'])
        self.norm_x_to_wkb(G_MIXPRE1)
        qpend = None
        for nb in range(4):
            s_ = self.wload(wq[nb], KC)
            for oc in range(4):
                j = nb * 4 + oc
                p_ = self.bank()
                fns = [lambda e, s_=s_, kc=kc, oc=oc, p_=p_: e.matmul(
                    self.ps[p_][:], self.ws[s_][:, kc, oc * 128:(oc + 1) * 128], self.wkb[:, kc, :],
                    start=(kc == 0), stop=(kc == KC - 1)) for kc in range(KC)]
                S.op('pe', fns, reads=[('ws', s_)] + self.WKB, writes=[('ps', p_)])
                if qpend is not None:
                    self.rope(qpend[1], self.hff[:, qpend[0], :], [('hf', qpend[0])])
                qpend = (j, p_)
        self.rope(qpend[1], self.hff[:, qpend[0], :], [('hf', qpend[0])])
        VB = [('vb', i, r) for i in range(2) for r in range(8)]
        KB = [('kb', i, r) for i in range(2) for r in range(8)]
        S.alias(self.WKB + self.WK(), VB + ['vones'])
        S.alias([('ws', 0)], KB)
        for i in range(2):
            S.op('dve', lambda e, i=i: e.memset(self.vb[i][:, :, 256:257], 1.0), writes=['vones'] + [('vb', i, r) for r in range(8)])
        S.alias([('sq', i) for i in range(4)] + [('tmp', 0), ('tmp', 1)], [('o0', q) for q in range(4)] + [('on', q) for q in range(4)])
        pinfo = {}

        def front(n, j, kc):
            vi = n % 2
            vbuf = self.vb[vi]
            sh = 2 * n + j
            ki = sh % 2
            kbuf = self.kb[ki]
            if kc == 0 and j == 0:
                for r in range(2):
                    for s2 in range(s + 1):
                        ix = r * (s + 1) + s2
                        S.dma('pool', lambda e, vbuf=vbuf, n=n, r=r, s2=s2, ix=ix: e.dma_start(
                            out=vbuf[:, ix * 4:(ix + 1) * 4, 0:256], in_=Vd[s2][r * NHEAD + n, :, :, :]),
                            reads=[('Vall', s2)], writes=[('vb', vi, ix)])
            if kc == 0:
                for r in range(2):
                    for s2 in range(s + 1):
                        ix = r * (s + 1) + s2
                        S.dma('pool', lambda e, kbuf=kbuf, sh=sh, r=r, s2=s2, ix=ix: e.dma_start(
                            out=kbuf[:, ix * T:(ix + 1) * T],
                            in_=Kd[s2][(r * NSUB + sh) * 128:(r * NSUB + sh + 1) * 128, :]),
                            reads=[('Kall', s2)], writes=[('kb', ki, ix)])
            stb = 4 + self.stbi % 3
            self.stbi += 1
            S.op('pe', lambda e, kbuf=kbuf, kc=kc, sh=sh, stb=stb: e.matmul(
                self.ps[stb][:], kbuf[:, kc * 128:(kc + 1) * 128], self.hff[:, sh, :], start=True, stop=True),
                reads=[('kb', ki, kc // 4), ('hf', sh)], writes=[('ps', stb)])
            pi = self.pti % 4
            self.pti += 1
            pinfo[(n, j, kc)] = pi
            S.op('act', lambda e, pi=pi, stb=stb: e.activation(out=self.pt[:, pi, :], in_=self.ps[stb][:], func=AF.Exp, scale=SCALE),
                 reads=[('ps', stb)], writes=[('pt', pi)])
            lc = kc % half
            mi = (lc - (half - 4)) + 4 * (kc // half) if lc >= half - 4 else -1
            if mi >= 0:
                S.op('dve', lambda e, pi=pi, mi=mi: e.tensor_tensor(out=self.pt[:, pi, :], in0=self.pt[:, pi, :], in1=self.maskb[:, mi, :], op=ALU.mult),
                     reads=[('pt', pi), 'cstA'], writes=[('pt', pi)])

        def back(n, j, kc):
            vi = n % 2
            vbuf = self.vb[vi]
            pi = pinfo[(n, j, kc)]
            fns = [lambda e, pi=pi, qs=qs, kc=kc, vbuf=vbuf: e.matmul(
                self.ps[qs][:, 0:257], self.pt[:, pi, qs * 128:(qs + 1) * 128], vbuf[:, kc, 0:257],
                start=(kc == 0), stop=(kc == nkc - 1)) for qs in range(4)]
            S.op('pe', fns, reads=[('pt', pi), ('vb', vi, kc // 4), 'vones'], writes=[('ps', q) for q in range(4)])
            if kc == nkc - 1:
                Q4 = range(4)
                if j == 0:
                    for qs in Q4:
                        S.op('dve', lambda e, qs=qs: e.reciprocal(out=self.sm[:, 8 + qs:9 + qs], in_=self.ps[qs][:, 256:257]),
                             reads=[('ps', qs)], writes=[('smr', qs)])
                    for qs in Q4:
                        S.op('act', lambda e, qs=qs: e.activation(out=self.o0[:, qs, :], in_=self.ps[qs][:, 0:256], func=AF.Copy,
                                                                 scale=self.sm[:, 8 + qs:9 + qs]),
                             reads=[('ps', qs), ('smr', qs)], writes=[('o0', qs)])
                else:
                    for qs in Q4:
                        S.op('dve', lambda e, qs=qs: e.reciprocal(out=self.sm[:, 12 + qs:13 + qs], in_=self.ps[qs][:, 256:257]),
                             reads=[('ps', qs)], writes=[('smr', 4 + qs)])
                    for qs in Q4:
                        S.op('dve', lambda e, qs=qs: e.tensor_tensor(out=self.sm[:, 12 + qs:13 + qs], in0=self.sm[:, 12 + qs:13 + qs], in1=self.sm[:, 0:1], op=ALU.mult),
                             reads=[('smr', 4 + qs), 'sm'], writes=[('smr', 4 + qs)])
                    for qs in Q4:
                        S.op('dve', lambda e, qs=qs: e.scalar_tensor_tensor(
                            out=self.o0[:, qs, :], in0=self.ps[qs][:, 0:256], scalar=self.sm[:, 12 + qs:13 + qs], in1=self.o0[:, qs, :],
                            op0=ALU.mult, op1=ALU.add), reads=[('ps', qs), ('smr', 4 + qs), ('o0', qs)], writes=[('o0', qs)])
                    for qs in Q4:
                        S.op('act', lambda e, qs=qs: e.activation(out=self.on[:, qs, :], in_=self.o0[:, qs, :], func=AF.Square,
                                                                 accum_out=self.sm[:, 16 + qs:17 + qs]),
                             reads=[('o0', qs)], writes=[('on', qs), ('smr', 8 + qs)])
                    for qs in Q4:
                        S.op('act', lambda e, qs=qs: e.activation(out=self.sm[:, 16 + qs:17 + qs], in_=self.sm[:, 16 + qs:17 + qs], func=AF.Sqrt,
                                                                 bias=self.eps[:], scale=1.0 / 256),
                             reads=[('smr', 8 + qs), 'eps'], writes=[('smr', 8 + qs)])
                    for qs in Q4:
                        S.op('dve', lambda e, qs=qs: e.reciprocal(out=self.sm[:, 16 + qs:17 + qs], in_=self.sm[:, 16 + qs:17 + qs]),
                             reads=[('smr', 8 + qs)], writes=[('smr', 8 + qs)])
                    for qs in Q4:
                        S.op('dve', lambda e, qs=qs: e.tensor_scalar_mul(out=self.sm[:, 16 + qs:17 + qs], in0=self.sm[:, 16 + qs:17 + qs],
                                                                        scalar1=1.0 - LAMBDA_INIT),
                             reads=[('smr', 8 + qs)], writes=[('smr', 8 + qs)])
                    for qs in Q4:
                        S.op('dve', lambda e, qs=qs: e.scalar_tensor_tensor(
                            out=self.on[:, qs, :], in0=self.o0[:, qs, :], scalar=self.sm[:, 16 + qs:17 + qs], in1=self.gsub[:],
                            op0=ALU.mult, op1=ALU.mult), reads=[('o0', qs), ('smr', 8 + qs), 'gsub'], writes=[('on', qs)])

                    def transposes(n=n):
                        for hv in range(2):
                            fns = [lambda e, qs=qs, hv=hv: e.transpose(
                                self.ps[7][:, qs * 128:(qs + 1) * 128], self.on[:, qs, hv * 128:(hv + 1) * 128], self.ident[:])
                                for qs in range(4)]
                            S.op('pe', fns, reads=[('on', qs) for qs in range(4)] + ['ident'], writes=[('ps', 7)])
                            S.op('act', lambda e, c=16 + 2 * n + hv: e.copy(out=self.hff[:, c, :], in_=self.ps[7][:]),
                                 reads=[('ps', 7)], writes=[('hf', 16 + 2 * n + hv)])
                    deferred.append([7, transposes])

        self.stbi = 0
        stages = [(n, j, kc) for n in range(NHEAD) for j in range(2) for kc in range(nkc)]
        deferred = []
        front(*stages[0])
        front(*stages[1])
        for t, st_ in enumerate(stages):
            if t + 2 < len(stages):
                front(*stages[t + 2])
            back(*st_)
            for d_ in deferred:
                d_[0] -= 1
            while deferred and deferred[0][0] <= 0:
                deferred.pop(0)[1]()
        while deferred:
            deferred.pop(0)[1]()
        S.alias([('o0', q) for q in range(4)] + [('on', q) for q in range(4)], [('sq', i) for i in range(4)] + [('tmp', 0), ('tmp', 1)])
        S.alias(VB + ['vones'], self.WK())
        S.alias(KB, [('ws', 0)])
        for nb in range(4):
            s_ = self.wload(wo[nb], KC)
            for oc in range(4):
                c = nb * 4 + oc
                p_ = self.bank()
                fns = [lambda e, s_=s_, kc=kc, oc=oc, p_=p_: e.matmul(
                    self.ps[p_][:], self.ws[s_][:, kc, oc * 128:(oc + 1) * 128], self.hff[:, 16 + kc, :],
                    start=(kc == 0), stop=(kc == KC - 1)) for kc in range(KC)]
                S.op('pe', fns, reads=[('ws', s_)] + [('hf', 16 + kc) for kc in range(KC)], writes=[('ps', p_)])
                S.op('act', lambda e, c=c, p_=p_: e.copy(out=self.wkm(c), in_=self.ps[p_][:]), reads=[('ps', p_)], writes=self.WK(c))
                self.ss_acc(self.ps[p_][:], [('ps', p_)], T, c)
        self.ss_fin(T, self.rstd, 'rstd')
        self.residual_add(G_MIXPOST1)
        self.ffn(1, wg, wu, wd, G_FFNPRE1, G_FFNPOST1, last=True)
        S.dma('sp', lambda e: e.dma_start(out=yT[s], in_=self.xres[:]), reads=self.X(), writes=[('yT', s)])


def _blk(W):
    K, N = W.shape
    return np.ascontiguousarray(W.reshape(K // 128, 128, N // 512, 512).transpose(2, 1, 0, 3))


def _vec(v):
    return np.ascontiguousarray(v.reshape(KC, 128).T)


def _tile_of(core, s):
    return TILES[core % 2][s]


def _const_tables():
    inv = THETA ** (-np.arange(0, ROT, 2, dtype=np.float32) / ROT)
    tabs = {}
    for par in range(2):
        invc = np.zeros((NT, 128, 4, T), np.float32)
        rope = np.zeros((NT, 128, 2, T), np.float32)
        mask = np.zeros((NT, 128, NMASK, T), np.float32)
        for s in range(NT):
            ti = TILES[par][s]
            pos = ti * T + np.arange(T)
            for g, w in enumerate((2, 4, 8, 16)):
                invc[s, :, g, :] = (1.0 / np.minimum(pos + 1, w).astype(np.float32))[None, :]
            ang = pos.astype(np.float32)[None, :] * inv[:, None]
            c, sn = np.cos(ang), np.sin(ang)
            rope[s, :, 0, :] = 1.0
            rope[s, 0:16, 0, :] = c
            rope[s, 16:32, 0, :] = c
            rope[s, 0:16, 1, :] = -sn
            rope[s, 16:32, 1, :] = sn
            for m in range(NMASK):
                kpos = TILES[m // 4][s] * T + (m % 4) * 128 + np.arange(128)
                mask[s, :, m, :] = (kpos[:, None] <= pos[None, :]).astype(np.float32)
        tabs[par] = (invc, rope, mask)
    R = np.zeros((128, 128), np.float32)
    for m in range(16):
        R[m + 16, m] = 1.0
        R[m, m + 16] = 1.0
    return tabs, R


_CACHE = {}


def _get_prog(mode):
    if mode not in _CACHE:
        p = Prog(mode)
        _CACHE[mode] = p.build_F()
    return _CACHE[mode]


def _prep(x, norm_mix_pre, norm_mix_post, norm_ffn_pre, norm_ffn_post, w_pool, pool_scale,
          kv_norm, w_kv, w_q, lambda_q1, lambda_k1, lambda_q2, lambda_k2, subln_gain, w_o,
          w_ffn_gate, w_ffn_up, w_ffn_down):
    f = lambda a: np.asarray(a, dtype=np.float32)
    P = {}
    P["x"] = f(x)
    P["tabs"], P["R"] = _const_tables()
    gains = np.stack([_vec(f(norm_mix_pre)[0]), _vec(f(norm_mix_post)[0]), _vec(f(norm_ffn_pre)[0]), _vec(f(norm_ffn_post)[0]),
                      _vec(f(pool_scale)[0]), _vec(f(kv_norm)),
                      _vec(f(norm_mix_pre)[1]), _vec(f(norm_mix_post)[1]), _vec(f(norm_ffn_pre)[1]), _vec(f(norm_ffn_post)[1])], axis=1)
    P["gains"] = np.ascontiguousarray(gains)
    P["wpool"] = np.ascontiguousarray(f(w_pool)[0].reshape(4, 4, 128, 512).transpose(0, 2, 1, 3))
    P["wgate"] = np.stack([_blk(f(w_ffn_gate)[l]) for l in range(2)])
    P["wup"] = np.stack([_blk(f(w_ffn_up)[l]) for l in range(2)])
    P["wdown"] = np.stack([_blk(f(w_ffn_down)[l]) for l in range(2)])
    P["wkv"] = _blk(f(w_kv))
    P["wq"] = _blk(f(w_q)[0])
    P["wo"] = _blk(f(w_o)[0])
    P["gsub"] = np.ascontiguousarray(np.broadcast_to(f(subln_gain)[0][None, :], (128, 256)))
    lamv = np.stack([f(lambda_q1)[0], f(lambda_k1)[0], f(lambda_q2)[0], f(lambda_k2)[0]])
    P["lamv"] = np.ascontiguousarray(np.broadcast_to(lamv[None], (128, 4, 128)))
    P["ident"] = np.eye(128, dtype=np.float32)
    return P


def kernel(**inputs):
    P = _prep(**inputs)
    x = P["x"]
    xpad = np.concatenate([np.zeros((BATCH, HALO, D), np.float32), x], axis=1)
    maps = []
    for c in range(NCORES):
        b, par = c // 2, c % 2
        xt = np.empty((NT, 128, KC, TW), np.float32)
        for s in range(NT):
            t0 = TILES[par][s] * T
            blk = xpad[b, t0:t0 + TW, :]
            xt[s] = blk.T.reshape(KC, 128, TW).transpose(1, 0, 2)
        invc, rope, mask = P["tabs"][par]
        maps.append({"xT": xt, "gains": P["gains"], "invc": invc, "rope": rope, "maskd": mask, "Rmat": P["R"],
                     "ident": P["ident"], "gsub": P["gsub"], "lamv": P["lamv"], "wpool": P["wpool"],
                     "wgate0": P["wgate"][0], "wup0": P["wup"][0], "wdown0": P["wdown"][0],
                     "wgate1": P["wgate"][1], "wup1": P["wup"][1], "wdown1": P["wdown"][1],
                     "wkv": P["wkv"], "wq": P["wq"], "wo": P["wo"]})
    res = run_bass_kernel_spmd(_get_prog('F'), maps, core_ids=list(range(NCORES))).results
    return _untile(res, "yT")


def _untile(res, name):
    out = np.empty((BATCH, SEQ, D), np.float32)
    for c in range(NCORES):
        b, par = c // 2, c % 2
        yT = np.asarray(res[c][name])
        for s in range(NT):
            t0 = TILES[par][s] * T
            out[b, t0:t0 + T, :] = yT[s].transpose(2, 1, 0).reshape(T, D)
    return out
```

```python
import math
from collections import defaultdict
from contextlib import ExitStack

import numpy as np
import concourse.bass as bass
import concourse.mybir as mybir
from concourse.bass_utils import run_bass_kernel_spmd

F32 = mybir.dt.float32
BF16 = mybir.dt.bfloat16
ALU = mybir.AluOpType
AF = mybir.ActivationFunctionType
AX = mybir.AxisListType

NCORES = 8
BATCH, SEQ, D, DFF = 4, 4096, 2048, 5632
KC, FC = D // 128, DFF // 128
T = 512
NT = 4
HALO = 16
TW = T + HALO
NSUB, NHEAD = 16, 8
EPS = 1e-6
ROT = 32
THETA = 500000.0
TILES = ([0, 3, 4, 7], [1, 2, 5, 6])
NKC = [8, 16, 24, 32]
NMASK = 8
LAMBDA_INIT = 0.8 - 0.6 * math.exp(-0.3 * 1)
SCALE = 128 ** -0.5
G_MIXPRE0, G_MIXPOST0, G_FFNPRE0, G_FFNPOST0, G_POOL, G_KV, G_MIXPRE1, G_MIXPOST1, G_FFNPRE1, G_FFNPOST1 = range(10)
NG = 10

SAME_ENGINE_SYNC = True
KSTOP = 99


class Sched:
    ENG = ('pe', 'act', 'dve', 'pool', 'sp')

    def __init__(self, nc, stack, ndma=8, ndma_sp=48):
        self.nc = nc
        self.prog = {e: [] for e in self.ENG}
        self.sem = {}
        for e in ('pe', 'act', 'dve', 'pool'):
            self.sem[e] = stack.enter_context(nc.semaphore("c_" + e))
        self.seq = defaultdict(int)
        self.ndma = {'sp': ndma_sp, 'pool': ndma}
        for q in ('sp', 'pool'):
            for i in range(self.ndma[q]):
                self.sem[('d', q, i)] = stack.enter_context(nc.semaphore(f"d_{q}_{i}"))
        self.dma_n = {'sp': 0, 'pool': 0}
        self.known = {e: defaultdict(int) for e in self.ENG}
        self.last_w = {}
        self.readers = defaultdict(dict)
        self.nwaits = defaultdict(int)
        self.nops = defaultdict(int)

    def _deps(self, reads, writes):
        deps = {}

        def add(k, v):
            if deps.get(k, 0) < v:
                deps[k] = v
        for r in reads:
            for k, v in self.last_w.get(r, {}).items():
                add(k, v)
        for w in writes:
            for k, v in self.last_w.get(w, {}).items():
                add(k, v)
            for k, v in self.readers[w].items():
                add(k, v)
        return deps

    def alias(self, old, new):
        deps = self._deps((), old)
        for k in new:
            d = dict(self.last_w.get(k, {}))
            for kk, v in deps.items():
                if d.get(kk, 0) < v:
                    d[kk] = v
            self.last_w[k] = d

    def _emit_waits(self, eng, deps):
        for k, v in deps.items():
            if k == eng and (eng == 'pe' or not SAME_ENGINE_SYNC):
                continue
            if self.known[eng][k] >= v:
                continue
            sem = self.sem[k]
            self.prog[eng].append(lambda e, sem=sem, v=v: e.wait_ge(sem, v))
            self.known[eng][k] = v
            self.nwaits[eng] += 1

    def _commit(self, k, v, reads, writes):
        for r in reads:
            if self.readers[r].get(k, 0) < v:
                self.readers[r][k] = v
        for w in writes:
            self.last_w[w] = {k: v}
            self.readers[w] = {}

    def op(self, eng, fns, reads=(), writes=()):
        if callable(fns):
            fns = [fns]
        self._emit_waits(eng, self._deps(reads, writes))
        self.seq[eng] += 1
        v = self.seq[eng]
        sem = self.sem[eng]
        for f in fns[:-1]:
            self.prog[eng].append(f)
        last = fns[-1]
        self.prog[eng].append(lambda e, f=last, sem=sem: f(e).then_inc(sem, 1))
        self.nops[eng] += len(fns)
        self._commit(eng, v, reads, writes)

    def dma(self, q, fn, reads=(), writes=()):
        j = self.dma_n[q]
        self.dma_n[q] += 1
        nd = self.ndma[q]
        assert q != 'sp' or j < nd, "HWDGE (sp) DMA semaphores are never reused"
        k = ('d', q, j % nd)
        v = 16 * (j // nd + 1)
        deps = self._deps(reads, writes)
        if v > 16:
            deps[k] = max(deps.get(k, 0), v - 16)
        self._emit_waits(q, deps)
        sem = self.sem[k]
        self.prog[q].append(lambda e, f=fn, sem=sem: f(e).then_inc(sem, 16))
        self._commit(k, v, reads, writes)

    def coll(self, stack, fn, reads=(), writes=()):
        n = sum(1 for k in self.sem if isinstance(k, tuple) and k[0] == 'cc')
        k = ('cc', n)
        self.sem[k] = stack.enter_context(self.nc.semaphore(f"cc_{n}"))
        self._emit_waits('pool', self._deps(reads, writes))
        sem = self.sem[k]
        self.prog['pool'].append(lambda e, f=fn, sem=sem: f(e).then_inc(sem))
        self._commit(k, 1, reads, writes)

    def final_wait(self, eng):
        deps = {}
        for q, n in self.dma_n.items():
            nd = self.ndma[q]
            for i in range(min(n, nd)):
                cnt = (n - 1 - i) // nd + 1
                deps[('d', q, i)] = 16 * cnt
        self._emit_waits(eng, deps)

    def emit(self, block):
        for name, deco in (('pe', block.tensor), ('act', block.scalar), ('dve', block.vector),
                           ('pool', block.gpsimd), ('sp', block.sync)):
            prog = self.prog[name]

            def body(e, prog=prog):
                for f in prog:
                    f(e)
            deco(body)


class Prog:
    def __init__(self, mode):
        self.mode = mode
        self.nc = bass.Bass("TRN2", target_bir_lowering=False)

    def din(self, name, shape, dt=F32):
        return self.nc.dram_tensor(name, list(shape), dt, kind="ExternalInput").ap()

    def dout(self, name, shape, dt=F32):
        return self.nc.dram_tensor(name, list(shape), dt, kind="ExternalOutput").ap()

    def alloc(self, st):
        nc = self.nc
        self.S = Sched(nc, st)
        A = lambda name, shape, dt: nc.alloc_sbuf_tensor('sb_' + name, shape, dt)
        self.xres = A("xres", [128, KC, T], F32)
        self.xh = A("xh", [128, KC, HALO], F32)
        self.wk = A("wk", [128, KC, TW], F32)
        self.hff = A("hff", [128, FC, T], BF16)
        self.ws = [A(f"ws{i}", [128, KC, T], BF16) for i in range(4)]
        self.gains = A("gains", [128, NG, KC], F32)
        self.rstd = A("rstd", [128, T], F32)
        self.rstdh = A("rstdh", [128, HALO], F32)
        self.sq = A("sq", [128, 4, T], BF16)
        self.pt = A("pt", [128, 4, T], BF16)
        self.tmp = A("tmp", [128, 2, T], F32)
        self.cstA = A("cstA", [128, 4, T], F32)
        self.cstB = A("cstB", [128, 2, T], F32)
        self.ones = A("ones", [128, 128], BF16)
        self.ident = A("ident", [128, 128], F32)
        self.Rm = A("Rm", [128, 128], BF16)
        self.gsub = A("gsub", [128, 256], F32)
        self.eps = A("eps", [128, 1], F32)
        self.sm = A("sm", [128, 32], F32)
        self.ps = [nc.alloc_psum_tensor(f"ps{i}", [128, T], F32) for i in range(8)]
        wkflat = self.wk[:].rearrange("p c t -> p (c t)")
        wkb16 = wkflat.bitcast(BF16)
        self.wkb = wkb16[:, 0:KC * T].rearrange("p (c t) -> p c t", c=KC)
        self.vb = [wkb16[:, i * 8448:(i + 1) * 8448].rearrange("p (c w) -> p c w", w=264) for i in range(2)]
        ws0 = self.ws[0][:].rearrange("p c t -> p (c t)")
        self.kb = [ws0[:, i * 4096:(i + 1) * 4096] for i in range(2)]
        self.maskb = self.cstA[:].rearrange("p c t -> p (c t)").bitcast(BF16).rearrange("p (c t) -> p c t", c=8)
        hf32 = self.hff[:].rearrange("p c t -> p (c t)").bitcast(F32)
        self.pt_a = hf32[:, 0:4 * TW].rearrange("p (c t) -> p c t", c=4)
        self.pt_b = hf32[:, 4 * TW:8 * TW].rearrange("p (c t) -> p c t", c=4)
        self.o0 = self.sq[:].rearrange("p c t -> p (c t)").bitcast(F32).rearrange("p (c t) -> p c t", c=4)
        self.on = self.tmp[:].rearrange("p c t -> p (c t)").rearrange("p (c t) -> p c t", c=4)
        self.wslot_rr = 0
        self.sqi = 0
        self.psi = 0
        self.pti = 0

    @staticmethod
    def X(c=None):
        return [('x', i) for i in range(KC)] if c is None else [('x', c)]

    @staticmethod
    def WK(c=None):
        return [('wk', i) for i in range(KC)] if c is None else [('wk', c)]

    WKB = ['wkb']
    PT = [('pt', i) for i in range(4)]

    def wload(self, src, nk, allowed=(0, 1, 2, 3)):
        while self.wslot_rr % 4 not in allowed:
            self.wslot_rr += 1
        s = self.wslot_rr % 4
        self.wslot_rr += 1
        dst = self.ws[s]
        self.S.dma('pool', lambda e: e.dma_start(out=dst[:, 0:nk, :], in_=src), writes=[('ws', s)])
        return s

    def bank(self, n=6):
        b = self.psi % n
        self.psi += 1
        return b

    def ss_acc(self, src, keys, N, c, nch=KC, scale=None):
        S = self.S
        i = self.sqi % 4
        self.sqi += 1
        if scale is None:
            S.op('act', lambda e: e.activation(out=self.sq[:, i, 0:N], in_=src, func=AF.Square),
                 reads=keys, writes=[('sq', i)])
        else:
            S.op('act', lambda e: e.activation(out=self.sq[:, i, 0:N], in_=src, func=AF.Square, scale=scale),
                 reads=keys + ['gains'], writes=[('sq', i)])
        S.op('pe', lambda e: e.matmul(self.ps[6][:, 0:N], self.ones[:], self.sq[:, i, 0:N], start=(c == 0), stop=(c == nch - 1)),
             reads=[('sq', i), 'ones'], writes=[('ps', 6)])

    def ss_fin(self, N, rstd, rkey, denom=D):
        S = self.S
        S.op('act', lambda e: e.activation(out=rstd[:, 0:N], in_=self.ps[6][:, 0:N], func=AF.Sqrt, bias=self.eps[:], scale=1.0 / denom),
             reads=[('ps', 6), 'eps'], writes=[rkey])
        S.op('dve', lambda e: e.reciprocal(out=rstd[:, 0:N], in_=rstd[:, 0:N]), reads=[rkey], writes=[rkey])

    def sumsq(self, src_fn, keys_fn, N, rstd, rkey, nch=KC, denom=D, scale_fn=None):
        for c in range(nch):
            self.ss_acc(src_fn(c), keys_fn(c), N, c, nch, None if scale_fn is None else scale_fn(c))
        self.ss_fin(N, rstd, rkey, denom)

    def rms_to(self, src_fn, skeys_fn, gidx, dst_fn, dkeys_fn, N, rstd, rkey):
        for c in range(KC):
            src, dst = src_fn(c), dst_fn(c)
            self.S.op('dve', lambda e, src=src, dst=dst, c=c: e.scalar_tensor_tensor(
                out=dst, in0=src, scalar=self.gains[:, gidx, c:c + 1], in1=rstd[:, 0:N], op0=ALU.mult, op1=ALU.mult),
                reads=skeys_fn(c) + [rkey, 'gains'], writes=dkeys_fn(c))

    def wkm(self, c):
        return self.wk[:, c, HALO:TW]

    def residual_add(self, gidx, want_ss=True):
        S = self.S
        for c in range(KC):
            S.op('dve', lambda e, c=c: e.scalar_tensor_tensor(
                out=self.wkm(c), in0=self.wkm(c), scalar=self.gains[:, gidx, c:c + 1], in1=self.rstd[:, 0:T],
                op0=ALU.mult, op1=ALU.mult), reads=self.WK(c) + ['rstd', 'gains'], writes=self.WK(c))
        for c in range(KC):
            S.op('dve', lambda e, c=c: e.tensor_tensor(out=self.xres[:, c, :], in0=self.xres[:, c, :], in1=self.wkm(c), op=ALU.add),
                 reads=self.WK(c) + self.X(c), writes=self.X(c))
        if want_ss:
            for c in range(KC):
                self.ss_acc(self.xres[:, c, :], self.X(c), T, c)
        self.x_ss_ready = want_ss

    def norm_x_to_wkb(self, gidx):
        S = self.S
        if getattr(self, 'x_ss_ready', False):
            self.ss_fin(T, self.rstd, 'rstd')
            self.x_ss_ready = False
        else:
            self.sumsq(lambda c: self.xres[:, c, :], lambda c: self.X(c), T, self.rstd, 'rstd')
        S.alias(self.WK(), self.WKB)
        self.rms_to(lambda c: self.xres[:, c, :], lambda c: self.X(c), gidx,
                    lambda c: self.wkb[:, c, :], lambda c: self.WKB, T, self.rstd, 'rstd')

    def ffn(self, l, wg, wu, wd, g_pre, g_post, last=False):
        S = self.S
        self.norm_x_to_wkb(g_pre)
        for b in range(FC // 4):
            sg = self.wload(wg[b], KC)
            su = self.wload(wu[b], KC)
            for fc in range(4):
                pa, pb = self.bank(), self.bank()
                for (s_, p_) in ((sg, pa), (su, pb)):
                    fns = [lambda e, s_=s_, p_=p_, kc=kc, fc=fc: e.matmul(
                        self.ps[p_][:], self.ws[s_][:, kc, fc * 128:(fc + 1) * 128], self.wkb[:, kc, :],
                        start=(kc == 0), stop=(kc == KC - 1)) for kc in range(KC)]
                    S.op('pe', fns, reads=[('ws', s_)] + self.WKB, writes=[('ps', p_)])
                ti = (b * 4 + fc) % 2
                S.op('act', lambda e, pa=pa, ti=ti: e.activation(out=self.tmp[:, ti, :], in_=self.ps[pa][:], func=AF.Silu),
                     reads=[('ps', pa)], writes=[('tmp', ti)])
                S.op('dve', lambda e, pb=pb, ti=ti, f=b * 4 + fc: e.tensor_tensor(
                    out=self.hff[:, f, :], in0=self.tmp[:, ti, :], in1=self.ps[pb][:], op=ALU.mult),
                    reads=[('tmp', ti), ('ps', pb)], writes=[('hf', b * 4 + fc)])
        S.alias(self.WKB, self.WK())
        parts = ((0, 16), (16, 32), (32, 44))
        for nb in range(4):
            banks = [0, 1, 2, 3] if nb % 2 == 0 else [4, 5, 6, 7]
            for pi, (k0, k1) in enumerate(parts):
                s_ = self.wload(wd[nb, :, k0:k1, :], k1 - k0)
                for dc in range(4):
                    fns = [lambda e, s_=s_, kk=kk, dc=dc, k0=k0, p_=banks[dc], pi=pi, k1=k1: e.matmul(
                        self.ps[p_][:], self.ws[s_][:, kk, dc * 128:(dc + 1) * 128], self.hff[:, k0 + kk, :],
                        start=(pi == 0 and kk == 0), stop=(pi == 2 and kk == k1 - k0 - 1)) for kk in range(k1 - k0)]
                    S.op('pe', fns, reads=[('ws', s_)] + [('hf', k0 + kk) for kk in range(k1 - k0)], writes=[('ps', banks[dc])])
            for dc in range(4):
                c = nb * 4 + dc
                S.op('act', lambda e, c=c, p_=banks[dc]: e.copy(out=self.wkm(c), in_=self.ps[p_][:]),
                     reads=[('ps', banks[dc])], writes=self.WK(c))
        self.sumsq(lambda c: self.wkm(c), lambda c: self.WK(c), T, self.rstd, 'rstd')
        self.residual_add(g_post, want_ss=not last)

    def rope(self, p_, dst, dkeys):
        S = self.S
        i = self.pti % 2
        self.pti += 1
        rb = self.pt[:, 2 + i, :]
        KR = '3'
        if KR == '3':
            S.op('dve', lambda e: e.tensor_copy(out=rb, in_=self.ps[p_][:]), reads=[('ps', p_)], writes=[('pt', 2 + i)])
        else:
            S.op('act', lambda e: e.copy(out=rb, in_=self.ps[p_][:]), reads=[('ps', p_)], writes=[('pt', 2 + i)])
        S.op('dve', lambda e: e.tensor_tensor(out=self.tmp[:, i, :], in0=self.ps[p_][:], in1=self.cstB[:, 0, :], op=ALU.mult),
             reads=[('ps', p_), 'cstB'], writes=[('tmp', i)])
        if KR != '2':
            S.op('pe', lambda e: e.matmul(self.ps[7][:], self.Rm[:], rb, start=True, stop=True),
                 reads=[('pt', 2 + i), 'Rm'], writes=[('ps', 7)])
        if KR == '4':
            S.op('act', lambda e: e.copy(out=dst, in_=self.ps[7][:]), reads=[('ps', 7)], writes=dkeys)
            return
        S.op('dve', lambda e: e.tensor_tensor(out=self.rstd[:, :], in0=self.ps[7][:], in1=self.cstB[:, 1, :], op=ALU.mult),
             reads=[('ps', 7), 'cstB'], writes=['rstd'])
        S.op('dve', lambda e: e.tensor_tensor(out=dst, in0=self.tmp[:, i, :], in1=self.rstd[:, :], op=ALU.add),
             reads=[('tmp', i), 'rstd'], writes=dkeys)

    def init_consts(self, gains_d, ident_d=None, R_d=None):
        S = self.S
        S.dma('sp', lambda e: e.dma_start(out=self.gains[:], in_=gains_d), writes=['gains'])
        S.op('dve', lambda e: e.memset(self.ones[:], 1.0), writes=['ones'])
        S.op('dve', lambda e: e.memset(self.eps[:], EPS), writes=['eps'])
        if ident_d is not None:
            S.dma('sp', lambda e: e.dma_start(out=self.ident[:], in_=ident_d), writes=['ident'])
        if R_d is not None:
            S.dma('pool', lambda e: e.dma_start(out=self.Rm[:], in_=R_d), writes=['Rm'])

    def build_A(self):
        nc = self.nc
        xT = self.din("xT", [NT, 128, KC, TW])
        gains_d = self.din("gains", [128, NG, KC])
        invc_d = self.din("invc", [NT, 128, 4, T])
        rope_d = self.din("rope", [NT, 128, 2, T])
        R_d = self.din("Rmat", [128, 128])
        wp = self.din("wpool", [4, 128, 4, T])
        wg = self.din("wgate", [FC // 4, 128, KC, T])
        wu = self.din("wup", [FC // 4, 128, KC, T])
        wd = self.din("wdown", [4, 128, FC, T])
        wkv = self.din("wkv", [8, 128, KC, T])
        x2 = self.dout("x2T", [NT, 128, KC, T])
        Ko = self.dout("Kout", [NT, NSUB, 128, T // 2]).bitcast(BF16)
        Vo = self.dout("Vout", [NT, NHEAD, 128, 4, 128]).bitcast(BF16)
        with ExitStack() as st:
            self.alloc(st)
            S = self.S
            self.init_consts(gains_d, None, R_d)
            for s in range(KNT):
                self.layer0_tile(s, xT, invc_d, rope_d, wp, wg, wu, wd, wkv, x2, Ko, Vo)
            S.final_wait('sp')
            with nc.Block() as block:
                S.emit(block)
        return nc

    def load_x0(self, s, xT, invc_d):
        S = self.S
        S.dma('sp', lambda e: e.dma_start(out=self.xres[:], in_=xT[s, :, :, HALO:TW]), writes=self.X())
        S.dma('sp', lambda e: e.dma_start(out=self.xh[:], in_=xT[s, :, :, 0:HALO]), writes=['xh'])
        S.dma('sp', lambda e: e.dma_start(out=self.cstA[:], in_=invc_d[s]), writes=['cstA'])

    def layer0_tile(self, s, xT, invc_d, rope_d, wp, wg, wu, wd, wkv, x2, Ko, Vo, after_kvnorm=None):
        S = self.S
        if s == 0:
            self.load_x0(s, xT, invc_d)
        self.sumsq(lambda c: self.xres[:, c, :], lambda c: self.X(c), T, self.rstd, 'rstd')
        self.rms_to(lambda c: self.xres[:, c, :], lambda c: self.X(c), G_MIXPRE0,
                    lambda c: self.wkm(c), lambda c: self.WK(c), T, self.rstd, 'rstd')
        self.sumsq(lambda c: self.xh[:, c, :], lambda c: ['xh'], HALO, self.rstdh, 'rstdh')
        self.rms_to(lambda c: self.xh[:, c, :], lambda c: ['xh'], G_MIXPRE0,
                    lambda c: self.wk[:, c, 0:HALO], lambda c: self.WK(c), HALO, self.rstdh, 'rstdh')
        if KSTOP <= 1:
            return self.dbg_out(s, x2)
        HT = [('hf', i) for i in range(17)]
        for g in range(4):
            h4 = self.wk[:, 4 * g:4 * g + 4, :]
            cur, curk = h4, [('wk', 4 * g + i) for i in range(4)]
            bufs = [(self.pt_a, 'pta'), (self.pt_b, 'ptb')]
            bi = 0
            for step in range(g + 1):
                sh = 1 << step
                dst, dk = bufs[bi]
                bi ^= 1
                S.op('dve', lambda e, dst=dst, cur=cur, sh=sh: e.tensor_tensor(
                    out=dst[:, :, sh:TW], in0=cur[:, :, sh:TW], in1=cur[:, :, 0:TW - sh], op=ALU.add),
                    reads=curk + HT, writes=[dk] + HT)
                cur, curk = dst, [dk]
            oth, ok = bufs[bi]
            for i in range(4):
                c = 4 * g + i
                S.op('dve', lambda e, cur=cur, i=i, c=c, g=g: e.scalar_tensor_tensor(
                    out=self.hff[:, 20 + c, :], in0=cur[:, i, HALO:TW], scalar=1.0 / (2 << g), in1=self.wkm(c),
                    op0=ALU.mult, op1=ALU.subtract),
                    reads=curk + self.WK(c) + HT, writes=[('hf', 20 + c)])
            for i in range(4):
                c = 4 * g + i
                S.op('dve', lambda e, oth=oth, cur=cur, i=i, g=g: e.tensor_tensor(
                    out=oth[:, i, HALO:2 * HALO], in0=cur[:, i, HALO:2 * HALO], in1=self.cstA[:, g, 0:HALO], op=ALU.mult),
                    reads=curk + ['cstA'] + HT, writes=[ok] + HT)
                S.op('dve', lambda e, oth=oth, i=i, c=c: e.tensor_tensor(
                    out=self.hff[:, 20 + c, 0:HALO], in0=oth[:, i, HALO:2 * HALO], in1=self.wk[:, c, HALO:2 * HALO], op=ALU.subtract),
                    reads=[ok] + self.WK(c) + HT, writes=[('hf', 20 + c)])
        if KSTOP <= 2:
            return self.dbg_out(s, x2)
        for g in range(4):
            s_ = self.wload(wp[g], 4)
            for oc in range(4):
                c = 4 * g + oc
                p_ = self.bank()
                fns = [lambda e, s_=s_, kc=kc, oc=oc, p_=p_, g=g: e.matmul(
                    self.ps[p_][:], self.ws[s_][:, kc, oc * 128:(oc + 1) * 128], self.hff[:, 20 + 4 * g + kc, :],
                    start=(kc == 0), stop=(kc == 3)) for kc in range(4)]
                S.op('pe', fns, reads=[('ws', s_)] + [('hf', 20 + 4 * g + kc) for kc in range(4)], writes=[('ps', p_)])
                S.op('act', lambda e, c=c, p_=p_: e.activation(out=self.wkm(c), in_=self.ps[p_][:], func=AF.Copy,
                                                              scale=self.gains[:, G_POOL, c:c + 1]),
                     reads=[('ps', p_), 'gains'], writes=self.WK(c))
                self.ss_acc(self.ps[p_][:], [('ps', p_)], T, c, scale=self.gains[:, G_POOL, c:c + 1])
        self.ss_fin(T, self.rstd, 'rstd')
        self.residual_add(G_MIXPOST0)
        if KSTOP <= 3:
            return self.dbg_out(s, x2)
        self.ffn(0, wg, wu, wd, G_FFNPRE0, G_FFNPOST0)
        S.dma('sp', lambda e: e.dma_start(out=x2[s], in_=self.xres[:]), reads=self.X(), writes=[('x2', s)])
        if KSTOP <= 5:
            return
        S.dma('sp', lambda e: e.dma_start(out=self.cstB[:], in_=rope_d[s]), writes=['cstB'])
        self.norm_x_to_wkb(G_KV)
        if after_kvnorm is not None:
            after_kvnorm()
        if KSTOP > 6:
            self.kvproj(s, wkv, Ko, Vo)
        S.alias(self.WKB, self.WK())


    def kvproj(self, s, wkv, Ko, Vo):
        S = self.S

        def krope(j, p_):
            si = j % 2
            self.rope(p_, self.pt[:, si, :], [('pt', si)])
            S.dma('pool', lambda e: e.dma_start(out=Ko[j * 128:(j + 1) * 128, :], in_=self.pt[:, si, :]),
                  reads=[('pt', si)], writes=[('Ko', s, j)])
        kpend = None
        slots = {0: self.wload(wkv[0], KC)}
        for blk in range(8):
            if blk + 1 < 8:
                slots[blk + 1] = self.wload(wkv[blk + 1], KC)
            s_ = slots[blk]
            if blk < 4:
                nb = blk
                for oc in range(4):
                    j = nb * 4 + oc
                    p_ = self.bank()
                    fns = [lambda e, s_=s_, kc=kc, oc=oc, p_=p_: e.matmul(
                        self.ps[p_][:], self.ws[s_][:, kc, oc * 128:(oc + 1) * 128], self.wkb[:, kc, :],
                        start=(kc == 0), stop=(kc == KC - 1)) for kc in range(KC)]
                    S.op('pe', fns, reads=[('ws', s_)] + self.WKB, writes=[('ps', p_)])
                    if kpend is not None:
                        krope(*kpend)
                    kpend = (j, p_)
                if blk == 3:
                    krope(*kpend)
            else:
                nb = blk - 4
                for tc in range(4):
                    p_ = self.bank()
                    fns = [lambda e, s_=s_, kc=kc, tc=tc, p_=p_: e.matmul(
                        self.ps[p_][:], self.wkb[:, kc, tc * 128:(tc + 1) * 128], self.ws[s_][:, kc, :],
                        start=(kc == 0), stop=(kc == KC - 1)) for kc in range(KC)]
                    S.op('pe', fns, reads=[('ws', s_)] + self.WKB, writes=[('ps', p_)])
                    si = tc % 2
                    S.op('act', lambda e, p_=p_, si=si: e.copy(out=self.pt[:, si, :], in_=self.ps[p_][:]),
                         reads=[('ps', p_)], writes=[('pt', si)])
                    S.dma('pool', lambda e, nb=nb, tc=tc, si=si: e.dma_start(
                        out=Vo[2 * nb:2 * nb + 2, :, tc, :].rearrange("h p d -> p h d"),
                        in_=self.pt[:, si, :].rearrange("p (h d) -> p h d", h=2)), reads=[('pt', si)], writes=[('Vo', s, nb, tc)])

    def dbg_out(self, s, x2):
        self.S.dma('sp', lambda e: e.dma_start(out=x2[s], in_=self.xres[:]), reads=self.X(), writes=[('x2', s)])

    def build_F(self):
        nc = self.nc
        xT = self.din("xT", [NT, 128, KC, TW])
        gains_d = self.din("gains", [128, NG, KC])
        invc_d = self.din("invc", [NT, 128, 4, T])
        rope_d = self.din("rope", [NT, 128, 2, T])
        mask_d = self.din("maskd", [NT, 128, NMASK, T])
        R_d = self.din("Rmat", [128, 128])
        ident_d = self.din("ident", [128, 128])
        gsub_d = self.din("gsub", [128, 256])
        lam_d = self.din("lamv", [128, 4, 128])
        wp = self.din("wpool", [4, 128, 4, T])
        wg = [self.din(f"wgate{l}", [FC // 4, 128, KC, T]) for l in range(2)]
        wu = [self.din(f"wup{l}", [FC // 4, 128, KC, T]) for l in range(2)]
        wd = [self.din(f"wdown{l}", [4, 128, FC, T]) for l in range(2)]
        wkv = self.din("wkv", [8, 128, KC, T])
        wq = self.din("wq", [4, 128, KC, T])
        wo = self.din("wo", [4, 128, KC, T])
        yT = self.dout("yT", [NT, 128, KC, T])
        x2 = nc.dram_tensor("x2s", [NT, 128, KC, T], F32).ap()
        Kloc = [nc.dram_tensor(f"Kloc{s}", [NSUB * 128, T], BF16).ap() for s in range(NT)]
        Vloc = [nc.dram_tensor(f"Vloc{s}", [NHEAD * 128, 4 * 256], BF16).ap() for s in range(NT)]
        Kall = [nc.dram_tensor(f"Kall{s}", [2 * NSUB * 128, T], BF16).ap() for s in range(NT)]
        Vall = [nc.dram_tensor(f"Vall{s}", [2 * NHEAD * 128, 4 * 256], BF16).ap() for s in range(NT)]
        Vloc_v = [v.rearrange("(h p) (c d) -> h p c d", p=128, d=256) for v in Vloc]
        Vall_v = [v.rearrange("(g p) (c d) -> g p c d", p=128, d=256) for v in Vall]
        groups = [[2 * i, 2 * i + 1] for i in range(NCORES // 2)]
        with ExitStack() as st:
            self.alloc(st)
            S = self.S
            self.init_consts(gains_d, ident_d, R_d)
            S.dma('sp', lambda e: e.dma_start(out=self.gsub[:], in_=gsub_d), writes=['gsub'])
            for s in range(NT):
                if s + 1 < NT:
                    nxt = lambda s=s: self.load_x0(s + 1, xT, invc_d)
                else:
                    nxt = lambda: S.dma('sp', lambda e: e.dma_start(out=self.xres[:], in_=x2[0]), reads=[('x2', 0)], writes=self.X())
                self.layer0_tile(s, xT, invc_d, rope_d, wp, wg[0], wu[0], wd[0], wkv, x2, Kloc[s], Vloc_v[s], after_kvnorm=nxt)
                kkeys = [('Ko', s, j) for j in range(NSUB)]
                vkeys = [('Vo', s, nb, tc) for nb in range(4) for tc in range(4)]
                S.coll(st, lambda e, s=s: e.collective_compute("AllGather", ALU.bypass, replica_groups=groups,
                                                               ins=[Kloc[s].opt()], outs=[Kall[s].opt()]), reads=kkeys, writes=[('Kall', s)])
                S.coll(st, lambda e, s=s: e.collective_compute("AllGather", ALU.bypass, replica_groups=groups,
                                                               ins=[Vloc[s].opt()],
                                                               outs=[Vall[s].opt()]), reads=vkeys, writes=[('Vall', s)])
            self.compute_lambda(lam_d)
            for s in range(NT):
                self.layer1_tile(s, x2, rope_d, mask_d, Kall, Vall_v, wq, wo, wg[1], wu[1], wd[1], yT)
            S.final_wait('sp')
            with nc.Block() as block:
                S.emit(block)
        return nc

    def build_B(self):
        nc = self.nc
        x2 = self.din("x2T", [NT, 128, KC, T])
        gains_d = self.din("gains", [128, NG, KC])
        rope_d = self.din("rope", [NT, 128, 2, T])
        mask_d = self.din("maskd", [NT, 128, NMASK, T])
        R_d = self.din("Rmat", [128, 128])
        ident_d = self.din("ident", [128, 128])
        gsub_d = self.din("gsub", [128, 256])
        lam_d = self.din("lamv", [128, 4, 128])
        Kd = self.din("Kd", [NSUB, 128, SEQ // 2]).bitcast(BF16)
        Vd = self.din("Vd", [NHEAD, 128, SEQ // 128, 128]).bitcast(BF16)
        wq = self.din("wq", [4, 128, KC, T])
        wo = self.din("wo", [4, 128, KC, T])
        wg = self.din("wgate", [FC // 4, 128, KC, T])
        wu = self.din("wup", [FC // 4, 128, KC, T])
        wd = self.din("wdown", [4, 128, FC, T])
        yT = self.dout("yT", [NT, 128, KC, T])
        with ExitStack() as st:
            self.alloc(st)
            S = self.S
            self.init_consts(gains_d, ident_d, R_d)
            S.dma('sp', lambda e: e.dma_start(out=self.gsub[:], in_=gsub_d), writes=['gsub'])
            self.compute_lambda(lam_d)
            for s in range(KNT):
                self.layer1_tile(s, x2, rope_d, mask_d, Kd, Vd, wq, wo, wg, wu, wd, yT)
            S.final_wait('sp')
            with nc.Block() as block:
                S.emit(block)
        return nc

    def compute_lambda(self, lam_d):
        S = self.S
        lv = self.cstB[:].rearrange("p c t -> p (c t)")[:, 0:512].rearrange("p (c t) -> p c t", c=4)
        S.dma('sp', lambda e: e.dma_start(out=lv, in_=lam_d), writes=['cstB'])
        for i in range(2):
            S.op('dve', lambda e, i=i: e.tensor_tensor(out=self.tmp[:, 0, i * 128:(i + 1) * 128], in0=lv[:, 2 * i, :], in1=lv[:, 2 * i + 1, :], op=ALU.mult),
                 reads=['cstB'], writes=[('tmp', 0)])
            S.op('dve', lambda e, i=i: e.reduce_sum(out=self.sm[:, 2 + i:3 + i], in_=self.tmp[:, 0, i * 128:(i + 1) * 128], axis=AX.X),
                 reads=[('tmp', 0)], writes=['sm'])
        S.op('act', lambda e: e.activation(out=self.sm[:, 4:6], in_=self.sm[:, 2:4], func=AF.Exp), reads=['sm'], writes=['sm'])
        S.op('dve', lambda e: e.tensor_tensor(out=self.sm[:, 0:1], in0=self.sm[:, 5:6], in1=self.sm[:, 4:5], op=ALU.subtract),
             reads=['sm'], writes=['sm'])
        S.op('dve', lambda e: e.tensor_scalar_add(out=self.sm[:, 0:1], in0=self.sm[:, 0:1], scalar1=-LAMBDA_INIT), reads=['sm'], writes=['sm'])

    def layer1_tile(self, s, x2, rope_d, mask_d, Kd, Vd, wq, wo, wg, wu, wd, yT):
        S = self.S
        nkc = NKC[s]
        half = nkc // 2
        if s > 0:
            S.dma('sp', lambda e: e.dma_start(out=self.xres[:], in_=x2[s]), reads=[('x2', s)], writes=self.X())
        S.dma('sp', lambda e: e.dma_start(out=self.cstB[:], in_=rope_d[s]), writes=['cstB'])
        S.dma('pool', lambda e: e.dma_start(out=self.maskb, in_=mask_d[s]), writes=['cstA'])
        self.norm_x_to_wkb(G_MIXPRE1)
        qpend = None
        for nb in range(4):
            s_ = self.wload(wq[nb], KC)
            for oc in range(4):
                j = nb * 4 + oc
                p_ = self.bank()
                fns = [lambda e, s_=s_, kc=kc, oc=oc, p_=p_: e.matmul(
                    self.ps[p_][:], self.ws[s_][:, kc, oc * 128:(oc + 1) * 128], self.wkb[:, kc, :],
                    start=(kc == 0), stop=(kc == KC - 1)) for kc in range(KC)]
                S.op('pe', fns, reads=[('ws', s_)] + self.WKB, writes=[('ps', p_)])
                if qpend is not None:
                    self.rope(qpend[1], self.hff[:, qpend[0], :], [('hf', qpend[0])])
                qpend = (j, p_)
        self.rope(qpend[1], self.hff[:, qpend[0], :], [('hf', qpend[0])])
        VB = [('vb', i, r) for i in range(2) for r in range(8)]
        KB = [('kb', i, r) for i in range(2) for r in range(8)]
        S.alias(self.WKB + self.WK(), VB + ['vones'])
        S.alias([('ws', 0)], KB)
        for i in range(2):
            S.op('dve', lambda e, i=i: e.memset(self.vb[i][:, :, 256:257], 1.0), writes=['vones'] + [('vb', i, r) for r in range(8)])
        S.alias([('sq', i) for i in range(4)] + [('tmp', 0), ('tmp', 1)], [('o0', q) for q in range(4)] + [('on', q) for q in range(4)])
        pinfo = {}

        def front(n, j, kc):
            vi = n % 2
            vbuf = self.vb[vi]
            sh = 2 * n + j
            ki = sh % 2
            kbuf = self.kb[ki]
            if kc == 0 and j == 0:
                for r in range(2):
                    for s2 in range(s + 1):
                        ix = r * (s + 1) + s2
                        S.dma('pool', lambda e, vbuf=vbuf, n=n, r=r, s2=s2, ix=ix: e.dma_start(
                            out=vbuf[:, ix * 4:(ix + 1) * 4, 0:256], in_=Vd[s2][r * NHEAD + n, :, :, :]),
                            reads=[('Vall', s2)], writes=[('vb', vi, ix)])
            if kc == 0:
                for r in range(2):
                    for s2 in range(s + 1):
                        ix = r * (s + 1) + s2
                        S.dma('pool', lambda e, kbuf=kbuf, sh=sh, r=r, s2=s2, ix=ix: e.dma_start(
                            out=kbuf[:, ix * T:(ix + 1) * T],
                            in_=Kd[s2][(r * NSUB + sh) * 128:(r * NSUB + sh + 1) * 128, :]),
                            reads=[('Kall', s2)], writes=[('kb', ki, ix)])
            stb = 4 + self.stbi % 3
            self.stbi += 1
            S.op('pe', lambda e, kbuf=kbuf, kc=kc, sh=sh, stb=stb: e.matmul(
                self.ps[stb][:], kbuf[:, kc * 128:(kc + 1) * 128], self.hff[:, sh, :], start=True, stop=True),
                reads=[('kb', ki, kc // 4), ('hf', sh)], writes=[('ps', stb)])
            pi = self.pti % 4
            self.pti += 1
            pinfo[(n, j, kc)] = pi
            S.op('act', lambda e, pi=pi, stb=stb: e.activation(out=self.pt[:, pi, :], in_=self.ps[stb][:], func=AF.Exp, scale=SCALE),
                 reads=[('ps', stb)], writes=[('pt', pi)])
            lc = kc % half
            mi = (lc - (half - 4)) + 4 * (kc // half) if lc >= half - 4 else -1
            if mi >= 0:
                S.op('dve', lambda e, pi=pi, mi=mi: e.tensor_tensor(out=self.pt[:, pi, :], in0=self.pt[:, pi, :], in1=self.maskb[:, mi, :], op=ALU.mult),
                     reads=[('pt', pi), 'cstA'], writes=[('pt', pi)])

        def back(n, j, kc):
            vi = n % 2
            vbuf = self.vb[vi]
            pi = pinfo[(n, j, kc)]
            fns = [lambda e, pi=pi, qs=qs, kc=kc, vbuf=vbuf: e.matmul(
                self.ps[qs][:, 0:257], self.pt[:, pi, qs * 128:(qs + 1) * 128], vbuf[:, kc, 0:257],
                start=(kc == 0), stop=(kc == nkc - 1)) for qs in range(4)]
            S.op('pe', fns, reads=[('pt', pi), ('vb', vi, kc // 4), 'vones'], writes=[('ps', q) for q in range(4)])
            if kc == nkc - 1:
                Q4 = range(4)
                if j == 0:
                    for qs in Q4:
                        S.op('dve', lambda e, qs=qs: e.reciprocal(out=self.sm[:, 8 + qs:9 + qs], in_=self.ps[qs][:, 256:257]),
                             reads=[('ps', qs)], writes=[('smr', qs)])
                    for qs in Q4:
                        S.op('act', lambda e, qs=qs: e.activation(out=self.o0[:, qs, :], in_=self.ps[qs][:, 0:256], func=AF.Copy,
                                                                 scale=self.sm[:, 8 + qs:9 + qs]),
                             reads=[('ps', qs), ('smr', qs)], writes=[('o0', qs)])
                else:
                    for qs in Q4:
                        S.op('dve', lambda e, qs=qs: e.reciprocal(out=self.sm[:, 12 + qs:13 + qs], in_=self.ps[qs][:, 256:257]),
                             reads=[('ps', qs)], writes=[('smr', 4 + qs)])
                    for qs in Q4:
                        S.op('dve', lambda e, qs=qs: e.tensor_tensor(out=self.sm[:, 12 + qs:13 + qs], in0=self.sm[:, 12 + qs:13 + qs], in1=self.sm[:, 0:1], op=ALU.mult),
                             reads=[('smr', 4 + qs), 'sm'], writes=[('smr', 4 + qs)])
                    for qs in Q4:
                        S.op('dve', lambda e, qs=qs: e.scalar_tensor_tensor(
                            out=self.o0[:, qs, :], in0=self.ps[qs][:, 0:256], scalar=self.sm[:, 12 + qs:13 + qs], in1=self.o0[:, qs, :],
                            op0=ALU.mult, op1=ALU.add), reads=[('ps', qs), ('smr', 4 + qs), ('o0', qs)], writes=[('o0', qs)])
                    for qs in Q4:
                        S.op('act', lambda e, qs=qs: e.activation(out=self.on[:, qs, :], in_=self.o0[:, qs, :], func=AF.Square,
                                                                 accum_out=self.sm[:, 16 + qs:17 + qs]),
                             reads=[('o0', qs)], writes=[('on', qs), ('smr', 8 + qs)])
                    for qs in Q4:
                        S.op('act', lambda e, qs=qs: e.activation(out=self.sm[:, 16 + qs:17 + qs], in_=self.sm[:, 16 + qs:17 + qs], func=AF.Sqrt,
                                                                 bias=self.eps[:], scale=1.0 / 256),
                             reads=[('smr', 8 + qs), 'eps'], writes=[('smr', 8 + qs)])
                    for qs in Q4:
                        S.op('dve', lambda e, qs=qs: e.reciprocal(out=self.sm[:, 16 + qs:17 + qs], in_=self.sm[:, 16 + qs:17 + qs]),
                             reads=[('smr', 8 + qs)], writes=[('smr', 8 + qs)])
                    for qs in Q4:
                        S.op('dve', lambda e, qs=qs: e.tensor_scalar_mul(out=self.sm[:, 16 + qs:17 + qs], in0=self.sm[:, 16 + qs:17 + qs],
                                                                        scalar1=1.0 - LAMBDA_INIT),
                             reads=[('smr', 8 + qs)], writes=[('smr', 8 + qs)])
                    for qs in Q4:
                        S.op('dve', lambda e, qs=qs: e.scalar_tensor_tensor(
                            out=self.on[:, qs, :], in0=self.o0[:, qs, :], scalar=self.sm[:, 16 + qs:17 + qs], in1=self.gsub[:],
                            op0=ALU.mult, op1=ALU.mult), reads=[('o0', qs), ('smr', 8 + qs), 'gsub'], writes=[('on', qs)])

                    def transposes(n=n):
                        for hv in range(2):
                            fns = [lambda e, qs=qs, hv=hv: e.transpose(
                                self.ps[7][:, qs * 128:(qs + 1) * 128], self.on[:, qs, hv * 128:(hv + 1) * 128], self.ident[:])
                                for qs in range(4)]
                            S.op('pe', fns, reads=[('on', qs) for qs in range(4)] + ['ident'], writes=[('ps', 7)])
                            S.op('act', lambda e, c=16 + 2 * n + hv: e.copy(out=self.hff[:, c, :], in_=self.ps[7][:]),
                                 reads=[('ps', 7)], writes=[('hf', 16 + 2 * n + hv)])
                    deferred.append([7, transposes])

        self.stbi = 0
        stages = [(n, j, kc) for n in range(NHEAD) for j in range(2) for kc in range(nkc)]
        deferred = []
        front(*stages[0])
        front(*stages[1])
        for t, st_ in enumerate(stages):
            if t + 2 < len(stages):
                front(*stages[t + 2])
            back(*st_)
            for d_ in deferred:
                d_[0] -= 1
            while deferred and deferred[0][0] <= 0:
                deferred.pop(0)[1]()
        while deferred:
            deferred.pop(0)[1]()
        S.alias([('o0', q) for q in range(4)] + [('on', q) for q in range(4)], [('sq', i) for i in range(4)] + [('tmp', 0), ('tmp', 1)])
        S.alias(VB + ['vones'], self.WK())
        S.alias(KB, [('ws', 0)])
        for nb in range(4):
            s_ = self.wload(wo[nb], KC)
            for oc in range(4):
                c = nb * 4 + oc
                p_ = self.bank()
                fns = [lambda e, s_=s_, kc=kc, oc=oc, p_=p_: e.matmul(
                    self.ps[p_][:], self.ws[s_][:, kc, oc * 128:(oc + 1) * 128], self.hff[:, 16 + kc, :],
                    start=(kc == 0), stop=(kc == KC - 1)) for kc in range(KC)]
                S.op('pe', fns, reads=[('ws', s_)] + [('hf', 16 + kc) for kc in range(KC)], writes=[('ps', p_)])
                S.op('act', lambda e, c=c, p_=p_: e.copy(out=self.wkm(c), in_=self.ps[p_][:]), reads=[('ps', p_)], writes=self.WK(c))
                self.ss_acc(self.ps[p_][:], [('ps', p_)], T, c)
        self.ss_fin(T, self.rstd, 'rstd')
        self.residual_add(G_MIXPOST1)
        self.ffn(1, wg, wu, wd, G_FFNPRE1, G_FFNPOST1, last=True)
        S.dma('sp', lambda e: e.dma_start(out=yT[s], in_=self.xres[:]), reads=self.X(), writes=[('yT', s)])


def _blk(W):
    K, N = W.shape
    return np.ascontiguousarray(W.reshape(K // 128, 128, N // 512, 512).transpose(2, 1, 0, 3))


def _vec(v):
    return np.ascontiguousarray(v.reshape(KC, 128).T)


def _tile_of(core, s):
    return TILES[core % 2][s]


def _const_tables():
    inv = THETA ** (-np.arange(0, ROT, 2, dtype=np.float32) / ROT)
    tabs = {}
    for par in range(2):
        invc = np.zeros((NT, 128, 4, T), np.float32)
        rope = np.zeros((NT, 128, 2, T), np.float32)
        mask = np.zeros((NT, 128, NMASK, T), np.float32)
        for s in range(NT):
            ti = TILES[par][s]
            pos = ti * T + np.arange(T)
            for g, w in enumerate((2, 4, 8, 16)):
                invc[s, :, g, :] = (1.0 / np.minimum(pos + 1, w).astype(np.float32))[None, :]
            ang = pos.astype(np.float32)[None, :] * inv[:, None]
            c, sn = np.cos(ang), np.sin(ang)
            rope[s, :, 0, :] = 1.0
            rope[s, 0:16, 0, :] = c
            rope[s, 16:32, 0, :] = c
            rope[s, 0:16, 1, :] = -sn
            rope[s, 16:32, 1, :] = sn
            for m in range(NMASK):
                kpos = TILES[m // 4][s] * T + (m % 4) * 128 + np.arange(128)
                mask[s, :, m, :] = (kpos[:, None] <= pos[None, :]).astype(np.float32)
        tabs[par] = (invc, rope, mask)
    R = np.zeros((128, 128), np.float32)
    for m in range(16):
        R[m + 16, m] = 1.0
        R[m, m + 16] = 1.0
    return tabs, R


_CACHE = {}


def _get_prog(mode):
    if mode not in _CACHE:
        p = Prog(mode)
        _CACHE[mode] = p.build_F()
    return _CACHE[mode]


def _prep(x, norm_mix_pre, norm_mix_post, norm_ffn_pre, norm_ffn_post, w_pool, pool_scale,
          kv_norm, w_kv, w_q, lambda_q1, lambda_k1, lambda_q2, lambda_k2, subln_gain, w_o,
          w_ffn_gate, w_ffn_up, w_ffn_down):
    f = lambda a: np.asarray(a, dtype=np.float32)
    P = {}
    P["x"] = f(x)
    P["tabs"], P["R"] = _const_tables()
    gains = np.stack([_vec(f(norm_mix_pre)[0]), _vec(f(norm_mix_post)[0]), _vec(f(norm_ffn_pre)[0]), _vec(f(norm_ffn_post)[0]),
                      _vec(f(pool_scale)[0]), _vec(f(kv_norm)),
                      _vec(f(norm_mix_pre)[1]), _vec(f(norm_mix_post)[1]), _vec(f(norm_ffn_pre)[1]), _vec(f(norm_ffn_post)[1])], axis=1)
    P["gains"] = np.ascontiguousarray(gains)
    P["wpool"] = np.ascontiguousarray(f(w_pool)[0].reshape(4, 4, 128, 512).transpose(0, 2, 1, 3))
    P["wgate"] = np.stack([_blk(f(w_ffn_gate)[l]) for l in range(2)])
    P["wup"] = np.stack([_blk(f(w_ffn_up)[l]) for l in range(2)])
    P["wdown"] = np.stack([_blk(f(w_ffn_down)[l]) for l in range(2)])
    P["wkv"] = _blk(f(w_kv))
    P["wq"] = _blk(f(w_q)[0])
    P["wo"] = _blk(f(w_o)[0])
    P["gsub"] = np.ascontiguousarray(np.broadcast_to(f(subln_gain)[0][None, :], (128, 256)))
    lamv = np.stack([f(lambda_q1)[0], f(lambda_k1)[0], f(lambda_q2)[0], f(lambda_k2)[0]])
    P["lamv"] = np.ascontiguousarray(np.broadcast_to(lamv[None], (128, 4, 128)))
    P["ident"] = np.eye(128, dtype=np.float32)
    return P


def kernel(**inputs):
    P = _prep(**inputs)
    x = P["x"]
    xpad = np.concatenate([np.zeros((BATCH, HALO, D), np.float32), x], axis=1)
    maps = []
    for c in range(NCORES):
        b, par = c // 2, c % 2
        xt = np.empty((NT, 128, KC, TW), np.float32)
        for s in range(NT):
            t0 = TILES[par][s] * T
            blk = xpad[b, t0:t0 + TW, :]
            xt[s] = blk.T.reshape(KC, 128, TW).transpose(1, 0, 2)
        invc, rope, mask = P["tabs"][par]
        maps.append({"xT": xt, "gains": P["gains"], "invc": invc, "rope": rope, "maskd": mask, "Rmat": P["R"],
                     "ident": P["ident"], "gsub": P["gsub"], "lamv": P["lamv"], "wpool": P["wpool"],
                     "wgate0": P["wgate"][0], "wup0": P["wup"][0], "wdown0": P["wdown"][0],
                     "wgate1": P["wgate"][1], "wup1": P["wup"][1], "wdown1": P["wdown"][1],
                     "wkv": P["wkv"], "wq": P["wq"], "wo": P["wo"]})
    res = run_bass_kernel_spmd(_get_prog('F'), maps, core_ids=list(range(NCORES))).results
    return _untile(res, "yT")


def _untile(res, name):
    out = np.empty((BATCH, SEQ, D), np.float32)
    for c in range(NCORES):
        b, par = c // 2, c % 2
        yT = np.asarray(res[c][name])
        for s in range(NT):
            t0 = TILES[par][s] * T
            out[b, t0:t0 + T, :] = yT[s].transpose(2, 1, 0).reshape(T, D)
    return out
```
